# Optimizing a Trainium2 kernel written in Bass

```python
import jax
import jax.numpy as jnp
from jax import lax
import numpy as np

D_MODEL = 4096
BATCH = 4
SEQ = 2048
DEPTH = 2
DEC_BATCH = 8
DEC_SEQ = 1
PAST_LEN = 16384
PAGE_SIZE = 128

D_MIX = D_MODEL
A_WIDTH = D_MIX // 4
A_HEADS = 8
A_HEAD_DIM = A_WIDTH // A_HEADS
A_PATTERNS = ((128, 1), (512, 4), (2048, 16))
A_WIN_MAX = 2048
B_WIDTH = D_MIX // 4
B_POOLS = (2, 4, 8, 16)
B_GROUPS = 4
B_GROUP_W = B_WIDTH // B_GROUPS
B_BUF = 15
C_WIDTH = D_MIX // 2
C_HEADS = 16
C_HEAD_DIM = C_WIDTH // C_HEADS
C_CONV = 4
C_CHUNK = 64
IN_SPLITS = (A_WIDTH, A_WIDTH, A_WIDTH, A_WIDTH, B_WIDTH, B_WIDTH,
             C_WIDTH, C_WIDTH, C_WIDTH, C_WIDTH, C_HEADS, C_HEADS)
D_IN = 4 * A_WIDTH + 2 * B_WIDTH + 4 * C_WIDTH + 2 * C_HEADS
RMS_EPS = 1e-6

kernel_name = 'hybrid_dilated_pool_gdn_step'


def rms_norm(x, gain):
    xf = x.astype(jnp.float32)
    y = xf * lax.rsqrt(jnp.mean(xf * xf, axis=-1, keepdims=True) + RMS_EPS)
    return (y * gain.astype(jnp.float32)).astype(x.dtype)


def l2_norm(x):
    return x * lax.rsqrt(jnp.sum(x * x, axis=-1, keepdims=True) + RMS_EPS)


def split_cols(z):
    outs, off = [], 0
    for n in IN_SPLITS:
        outs.append(z[..., off:off + n])
        off += n
    return outs


def dilated_pattern_prompt(q, k, v, window, dilation):
    b, s, h, dh = q.shape
    n = window // dilation
    s_pad = -(-s // window) * window
    nb = s_pad // window

    def strided(t):
        t = jnp.pad(t, ((0, 0), (0, s_pad - s), (0, 0), (0, 0)))
        return t.reshape(b, nb, n, dilation, h, dh)

    def with_prev(t):
        prev = jnp.pad(t, ((0, 0), (1, 0), (0, 0), (0, 0), (0, 0), (0, 0)))[:, :-1]
        return jnp.concatenate([prev, t], axis=2)

    qs = strided(q)
    kk, vv = with_prev(strided(k)), with_prev(strided(v))
    sc = jnp.einsum('bnqrhd,bnkrhd->bnrhqk', qs, kk, preferred_element_type=jnp.float32) * (dh ** -0.5)
    qi = jnp.arange(n)[:, None]
    kj = jnp.arange(2 * n)[None, :]
    dist = qi + n - kj
    band = (dist >= 0) & (dist <= n)
    mask = band[None] & ((jnp.arange(nb)[:, None, None] > 0) | (kj[None] >= n))
    sc = jnp.where(mask[None, :, None, None], sc, -jnp.inf)
    m = jnp.max(sc, axis=-1)
    p = jnp.exp(sc - m[..., None])
    l = jnp.sum(p, axis=-1)
    o = jnp.einsum('bnrhqk,bnkrhd->bnqrhd', p.astype(v.dtype), vv, preferred_element_type=jnp.float32)
    o = o.reshape(b, s_pad, h, dh)[:, :s]

    def to_pos(t):
        return t.transpose(0, 1, 4, 2, 3).reshape(b, s_pad, h)[:, :s]

    return o, to_pos(m), to_pos(l)


def dilated_pattern_sample(q, k_all, v_all, n_prev, window, dilation):
    b, t, h, dh = q.shape
    n = window // dilation
    idx = n_prev + jnp.arange(t)[:, None] - dilation * jnp.arange(n + 1)[None, :]
    valid = idx >= 0
    idx = jnp.maximum(idx, 0)
    kg = k_all[:, idx]
    vg = v_all[:, idx]
    sc = jnp.einsum('bthd,btkhd->bthk', q, kg, preferred_element_type=jnp.float32) * (dh ** -0.5)
    sc = jnp.where(valid[None, :, None, :], sc, -jnp.inf)
    m = jnp.max(sc, axis=-1)
    p = jnp.exp(sc - m[..., None])
    l = jnp.sum(p, axis=-1)
    o = jnp.einsum('bthk,btkhd->bthd', p.astype(vg.dtype), vg, preferred_element_type=jnp.float32)
    return o, m, l


def combine_patterns(parts):
    m_all = parts[0][1]
    for _, m, _ in parts[1:]:
        m_all = jnp.maximum(m_all, m)
    num, den = 0.0, 0.0
    for o, m, l in parts:
        wgt = jnp.exp(m - m_all)
        num = num + o * wgt[..., None]
        den = den + l * wgt
    return num / den[..., None]


def pool_mix(u_ext, n_prev, start_pos, pool_w, pool_scale):
    b, total, _ = u_ext.shape
    t = total - n_prev
    uf = u_ext.astype(jnp.float32)
    csum = jnp.concatenate([jnp.zeros_like(uf[:, :1]), jnp.cumsum(uf, axis=1)], axis=1)
    rows = n_prev + jnp.arange(t)
    first = max(0, -start_pos)
    outs = []
    for gi, w in enumerate(B_POOLS):
        sl = slice(gi * B_GROUP_W, (gi + 1) * B_GROUP_W)
        c = csum[..., sl]
        lo = jnp.maximum(rows + 1 - w, first)
        mean = (c[:, rows + 1] - c[:, lo]) / (rows + 1 - lo).astype(jnp.float32)[None, :, None]
        outs.append(mean - uf[:, n_prev:, sl])
    y = jnp.stack(outs, axis=2)
    y = jnp.einsum('btgc,gcd->btgd', y, pool_w.astype(jnp.float32)).reshape(b, t, B_WIDTH)
    return y * pool_scale.astype(jnp.float32)


def short_conv(x_ext, conv_w):
    t = x_ext.shape[1] - (C_CONV - 1)
    y = x_ext[:, 0:t] * conv_w[0]
    for i in range(1, C_CONV):
        y = y + x_ext[:, i:i + t] * conv_w[i]
    return jax.nn.silu(y)


def delta_inputs(qkv, a, bg, a_log, dt_bias):
    b, t, _ = qkv.shape
    qkv = qkv.astype(jnp.float32).reshape(b, t, 3, C_HEADS, C_HEAD_DIM)
    q = l2_norm(qkv[:, :, 0]) * (C_HEAD_DIM ** -0.5)
    k = l2_norm(qkv[:, :, 1])
    v = qkv[:, :, 2]
    beta = jax.nn.sigmoid(bg.astype(jnp.float32))
    g = -jnp.exp(a_log.astype(jnp.float32)) * jax.nn.softplus(a.astype(jnp.float32) + dt_bias.astype(jnp.float32))
    return q, k, v, g, beta


def gated_delta_chunked(q, k, v, g, beta, s0):
    b, t, h, dk = q.shape
    dv = v.shape[-1]
    c = C_CHUNK
    n = t // c

    def chunks(x):
        x = x.reshape((b, n, c, h) + x.shape[3:])
        return jnp.moveaxis(x, (1, 3), (0, 2))

    qc, kc, vc = chunks(q), chunks(k), chunks(v)
    gc = jnp.cumsum(chunks(g), axis=-1)
    bc = chunks(beta)
    causal = jnp.arange(c)[:, None] >= jnp.arange(c)[None, :]
    strict = jnp.arange(c)[:, None] > jnp.arange(c)[None, :]
    decay = jnp.exp(jnp.where(causal, gc[..., :, None] - gc[..., None, :], -jnp.inf))
    kb = kc * bc[..., None]
    a_mat = jnp.where(strict, jnp.einsum('nbhid,nbhjd->nbhij', kb, kc) * decay, 0.0)
    lhs = a_mat + jnp.eye(c, dtype=a_mat.dtype)
    rhs = jnp.concatenate([vc * bc[..., None], kb * jnp.exp(gc)[..., None]], axis=-1)
    sol = lax.linalg.triangular_solve(lhs, rhs, left_side=True, lower=True, unit_diagonal=True)
    u, w = sol[..., :dv], sol[..., dv:]
    qk = jnp.einsum('nbhid,nbhjd->nbhij', qc, kc) * decay

    def step(state, xs):
        q_i, k_i, u_i, w_i, qk_i, g_i = xs
        v_new = u_i - jnp.einsum('bhck,bhkv->bhcv', w_i, state)
        o = (jnp.einsum('bhck,bhkv->bhcv', q_i * jnp.exp(g_i)[..., None], state)
             + jnp.einsum('bhij,bhjv->bhiv', qk_i, v_new))
        g_last = g_i[..., -1:]
        state = (state * jnp.exp(g_last)[..., None]
                 + jnp.einsum('bhck,bhcv->bhkv', k_i * jnp.exp(g_last - g_i)[..., None], v_new))
        return state, o

    state, o = lax.scan(step, s0, (qc, kc, u, w, qk, gc))
    o = jnp.moveaxis(o, (0, 2), (1, 3)).reshape(b, t, h, dv)
    return o, state


def gated_delta_recurrent(q, k, v, g, beta, s0):
    def step(state, xs):
        q_t, k_t, v_t, g_t, b_t = xs
        state = state * jnp.exp(g_t)[..., None, None]
        delta = (v_t - jnp.einsum('bhk,bhkv->bhv', k_t, state)) * b_t[..., None]
        state = state + jnp.einsum('bhk,bhv->bhkv', k_t, delta)
        return state, jnp.einsum('bhk,bhkv->bhv', q_t, state)

    xs = tuple(jnp.moveaxis(a, 1, 0) for a in (q, k, v, g, beta))
    state, o = lax.scan(step, s0, xs)
    return jnp.moveaxis(o, 0, 1), state


def trunk_layer(x, past, is_prompt, norm_w, w_in, conv_w, a_log, dt_bias, delta_norm_w, pool_w, pool_scale, w_out):
    b, t, _ = x.shape
    h = rms_norm(x, norm_w)
    z = h @ w_in
    qa, ka, va, ga, ub, gb, qc, kc, vc, gc, ac, bc = split_cols(z)
    qa = qa.reshape(b, t, A_HEADS, A_HEAD_DIM)
    ka = ka.reshape(b, t, A_HEADS, A_HEAD_DIM)
    va = va.reshape(b, t, A_HEADS, A_HEAD_DIM)
    qkv_c = jnp.concatenate([qc, kc, vc], axis=-1)
    if is_prompt:
        parts = [dilated_pattern_prompt(qa, ka, va, w, d) for w, d in A_PATTERNS]
        keep = min(A_WIN_MAX, t)
        new_k, new_v = ka[:, t - keep:], va[:, t - keep:]
        u_ext, n_prev, start = ub, 0, 0
        conv_ext = jnp.pad(qkv_c, ((0, 0), (C_CONV - 1, 0), (0, 0)))
        s0 = jnp.zeros((b, C_HEADS, C_HEAD_DIM, C_HEAD_DIM), jnp.float32)
    else:
        win_k, win_v, pool_buf, conv_buf, s0 = past
        n_buf = win_k.shape[1]
        k_all = jnp.concatenate([win_k.astype(ka.dtype), ka], axis=1)
        v_all = jnp.concatenate([win_v.astype(va.dtype), va], axis=1)
        parts = [dilated_pattern_sample(qa, k_all, v_all, n_buf, w, d) for w, d in A_PATTERNS]
        keep = min(A_WIN_MAX, n_buf + t)
        new_k, new_v = k_all[:, n_buf + t - keep:], v_all[:, n_buf + t - keep:]
        n_prev = pool_buf.shape[1]
        u_ext, start = jnp.concatenate([pool_buf.astype(ub.dtype), ub], axis=1), PAST_LEN - n_prev
        conv_ext = jnp.concatenate([conv_buf.astype(qkv_c.dtype), qkv_c], axis=1)
        s0 = s0.astype(jnp.float32)
    oa = combine_patterns(parts).reshape(b, t, A_WIDTH)
    ob = pool_mix(u_ext, n_prev, start, pool_w, pool_scale)
    new_pool = u_ext[:, u_ext.shape[1] - B_BUF:]
    new_conv = conv_ext[:, conv_ext.shape[1] - (C_CONV - 1):]
    q, k, v, g, beta = delta_inputs(short_conv(conv_ext, conv_w), ac, bc, a_log, dt_bias)
    if is_prompt:
        oc, s_new = gated_delta_chunked(q, k, v, g, beta, s0)
    else:
        oc, s_new = gated_delta_recurrent(q, k, v, g, beta, s0)
    oc = rms_norm(oc, delta_norm_w).reshape(b, t, C_WIDTH)
    mix = jnp.concatenate([oa * jax.nn.silu(ga), ob * jax.nn.silu(gb), oc * jax.nn.silu(gc)], axis=-1)
    y = x + mix.astype(x.dtype) @ w_out
    return y, (new_k, new_v, new_pool, new_conv, s_new)


def setup_inputs(seed: int = 0) -> dict:
    key = jax.random.key(seed)
    ks = jax.random.split(key, 18)
    f32 = jnp.float32
    a_buf = min(A_WIN_MAX, PAST_LEN)

    def nrm(k, shape, scale=1.0):
        return jax.random.normal(k, shape, f32) * scale

    return {
        'x_prompt': nrm(ks[0], (BATCH, SEQ, D_MODEL)),
        'x_sample': nrm(ks[1], (DEC_BATCH, DEC_SEQ, D_MODEL)),
        'cache_win_k': nrm(ks[2], (DEPTH, DEC_BATCH, a_buf, A_HEADS, A_HEAD_DIM)),
        'cache_win_v': nrm(ks[3], (DEPTH, DEC_BATCH, a_buf, A_HEADS, A_HEAD_DIM)),
        'state_pool': nrm(ks[4], (DEPTH, DEC_BATCH, B_BUF, B_WIDTH)),
        'state_conv': nrm(ks[5], (DEPTH, DEC_BATCH, C_CONV - 1, 3 * C_WIDTH)),
        'state_delta': nrm(ks[6], (DEPTH, DEC_BATCH, C_HEADS, C_HEAD_DIM, C_HEAD_DIM), 0.1),
        'norm_w': 1.0 + nrm(ks[7], (DEPTH, D_MODEL), 0.02),
        'w_in': nrm(ks[8], (DEPTH, D_MODEL, D_IN), D_MODEL ** -0.5),
        'conv_w': nrm(ks[9], (DEPTH, C_CONV, 3 * C_WIDTH), C_CONV ** -0.5),
        'a_log': jnp.log(jax.random.uniform(ks[10], (DEPTH, C_HEADS), f32, 1.0, 16.0)),
        'dt_bias': nrm(ks[11], (DEPTH, C_HEADS), 0.1),
        'delta_norm_w': 1.0 + nrm(ks[12], (DEPTH, C_HEAD_DIM), 0.02),
        'pool_w': nrm(ks[13], (DEPTH, B_GROUPS, B_GROUP_W, B_GROUP_W), B_GROUP_W ** -0.5),
        'pool_scale': 1.0 + nrm(ks[14], (DEPTH, B_WIDTH), 0.02),
        'w_out': nrm(ks[15], (DEPTH, D_MIX, D_MODEL), D_MIX ** -0.5),
        'final_norm_w': 1.0 + nrm(ks[16], (D_MODEL,), 0.02),
    }


def reference(x_prompt, x_sample, cache_win_k, cache_win_v, state_pool, state_conv, state_delta,
              norm_w, w_in, conv_w, a_log, dt_bias, delta_norm_w, pool_w, pool_scale, w_out, final_norm_w):
    hp, hs = x_prompt, x_sample
    sp_all, ss_all = [], []
    for l in range(DEPTH):
        lw = (norm_w[l], w_in[l], conv_w[l], a_log[l], dt_bias[l], delta_norm_w[l], pool_w[l], pool_scale[l], w_out[l])
        hp, sp = trunk_layer(hp, None, True, *lw)
        past = (cache_win_k[l], cache_win_v[l], state_pool[l], state_conv[l], state_delta[l])
        hs, ss = trunk_layer(hs, past, False, *lw)
        sp_all.append(sp)
        ss_all.append(ss)
    y_prompt = rms_norm(hp, final_norm_w)
    y_sample = rms_norm(hs, final_norm_w)

    def stack(states, i):
        return jnp.stack([s[i] for s in states], axis=0)

    return (y_prompt, y_sample,
            stack(sp_all, 0), stack(sp_all, 1), stack(sp_all, 2), stack(sp_all, 3), stack(sp_all, 4),
            stack(ss_all, 0), stack(ss_all, 1), stack(ss_all, 2), stack(ss_all, 3), stack(ss_all, 4))
```

```python
import contextlib
import os
import numpy as np
import concourse.bass as bass
import concourse.mybir as mybir
from concourse.bass_utils import run_bass_kernel_spmd

F32 = mybir.dt.float32
BF16 = mybir.dt.bfloat16
AF = mybir.ActivationFunctionType
ALU = mybir.AluOpType
AX = mybir.AxisListType

T = 2048
D = 4096
KC = 32
DIN = 14368
EPS = 1e-6
NEG = -30000.0
SCALE_A = 128 ** -0.5

C_ID = 0
C_ONES = 128
C_TRI = 256
C_MNI = 384
C_MNS = 512
C_MASK = 640
C_DM = C_MASK + 19 * 128
C_PCOEF = C_DM + 12 * 128
NCONST = C_PCOEF + 4


def make_consts():
    c = np.zeros((128, NCONST), np.float32)
    j = np.arange(128)[:, None]
    i = np.arange(128)[None, :]
    c[:, C_ID:C_ID + 128] = (i == j)
    c[:, C_ONES:C_ONES + 128] = 1.0
    c[:, C_TRI:C_TRI + 128] = (j <= i)
    c[:, C_MNI:C_MNI + 128] = np.where(i >= j, 0.0, NEG)
    c[:, C_MNS:C_MNS + 128] = np.where(i > j, 0.0, NEG)
    for blk in range(19):
        dlt = blk - 3
        dist = dlt * 128 + i - j
        m = ((dist >= 0) & (dist <= 128)).astype(np.float32)
        m += ((dist >= 0) & (dist <= 512) & (dist % 4 == 0))
        m += ((dist >= 0) & (dist <= 2048) & (dist % 16 == 0))
        c[:, C_MASK + blk * 128:C_MASK + (blk + 1) * 128] = m
    for g, w in enumerate((2, 4, 8, 16)):
        s = j
        t = i
        cur = ((s <= t) & (s > t - w)).astype(np.float32) / w - (s == t)
        prev = ((s - 128 <= t) & (s - 128 > t - w)).astype(np.float32) / w
        cnt = np.minimum(w, t + 1).astype(np.float32)
        first = ((s <= t) & (s > t - w)).astype(np.float32) / cnt - (s == t)
        c[:, C_DM + (g * 3 + 0) * 128:C_DM + (g * 3 + 1) * 128] = cur
        c[:, C_DM + (g * 3 + 1) * 128:C_DM + (g * 3 + 2) * 128] = prev
        c[:, C_DM + (g * 3 + 2) * 128:C_DM + (g * 3 + 3) * 128] = first
        for s_ in range(16):
            c[s_, C_PCOEF + g] = (1.0 / w if s_ >= 16 - w else 0.0) - (1.0 if s_ == 15 else 0.0)
    return c


class Res:
    __slots__ = ("w", "r")

    def __init__(self):
        self.w = None
        self.r = []


class Tl:
    def __init__(self, t):
        self.t = t
        self.r = Res()


class Sched:
    COMPUTE = ("pe", "act", "dve", "pool")
    QUEUES = ("sp", "pool")

    def __init__(self, nc, es, n_dma_sems=20, n_spare=24):
        self.nc = nc
        self.names = ("pe", "act", "dve", "pool", "sp")
        self.streams = {k: [] for k in self.names}
        self.count = {k: 0 for k in self.COMPUTE}
        self.epoch = {k: 0 for k in self.COMPUTE}
        self.known = {k: {} for k in self.names}
        self.n_dma_sems = n_dma_sems
        self.dma_rr = {q: 0 for q in self.QUEUES}
        self.dma_cnt = {}
        self.sems = {}
        self.spare = [es.enter_context(nc.semaphore("sx%d" % i)) for i in range(n_spare)]
        self.bar = es.enter_context(nc.semaphore("bar"))
        self.nbar = 0
        for q in self.QUEUES:
            for i in range(n_dma_sems):
                self.sems["d_%s_%d" % (q, i)] = es.enter_context(nc.semaphore("d_%s_%d" % (q, i)))
        for k in self.COMPUTE:
            self.sems[k + "#0"] = es.enter_context(nc.semaphore("c_" + k))

    def ckey(self, eng):
        return "%s#%d" % (eng, self.epoch[eng])

    def _deps(self, eng, reads, writes):
        evs = {}
        pek = self.ckey("pe")

        def add(ev):
            if ev is None:
                return
            k, v = ev
            if eng == "pe" and k == pek:
                return
            if evs.get(k, 0) < v:
                evs[k] = v
        for r in reads:
            add(r.w)
        for w in writes:
            add(w.w)
            for e in w.r:
                add(e)
        out = []
        kn = self.known[eng]
        for k, v in evs.items():
            if kn.get(k, 0) < v:
                kn[k] = v
                out.append((k, v))
        return out

    def _commit(self, ev, reads, writes):
        for w in writes:
            w.w = ev
            w.r = []
        for r in reads:
            if r not in writes:
                r.r.append(ev)
                if len(r.r) > 48:
                    best = {}
                    for k, v in r.r:
                        if best.get(k, 0) < v:
                            best[k] = v
                    r.r = list(best.items())

    def op(self, eng, fn, reads=(), writes=()):
        waits = self._deps(eng, reads, writes)
        self.count[eng] += 1
        k = self.ckey(eng)
        ev = (k, self.count[eng])
        self.streams[eng].append((waits, fn, (k, 1)))
        self._commit(ev, reads, writes)
        return ev

    def dma(self, q, fn, reads=(), writes=()):
        i = self.dma_rr[q]
        self.dma_rr[q] = (i + 1) % self.n_dma_sems
        key = "d_%s_%d" % (q, i)
        n = self.dma_cnt.get(key, 0)
        waits = self._deps(q, reads, writes)
        if n > 0 and self.known[q].get(key, 0) < 16 * n:
            self.known[q][key] = 16 * n
            waits.append((key, 16 * n))
        self.dma_cnt[key] = n + 1
        ev = (key, 16 * (n + 1))
        self.streams[q].append((waits, fn, (key, 16)))
        self._commit(ev, reads, writes)
        return ev

    def barrier(self):
        allv = {}
        for k in self.COMPUTE:
            if self.count[k]:
                allv[self.ckey(k)] = self.count[k]
        for key, n in self.dma_cnt.items():
            allv[key] = 16 * n
        waits = []
        for k, v in allv.items():
            if self.known["sp"].get(k, 0) < v:
                waits.append((k, v))
        self.nbar += 1
        nb = self.nbar
        self.streams["sp"].append((waits, None, ("__bar", 1)))
        for e in self.names:
            if e != "sp":
                self.streams[e].append(([("__bar", nb)], None, None))
            for k, v in allv.items():
                self.known[e][k] = v
        for k in self.COMPUTE:
            if self.count[k] > 12000:
                self.epoch[k] += 1
                self.count[k] = 0
                self.sems[self.ckey(k)] = self.spare.pop()

    def flush(self):
        nc = self.nc
        sems = self.sems
        sems["__bar"] = self.bar
        streams = self.streams
        self.streams = {k: [] for k in self.names}

        def run(engh, name):
            for waits, fn, inc in streams[name]:
                for k, v in waits:
                    engh.wait_ge(sems[k], v)
                if fn is None:
                    if inc is not None:
                        engh.sem_inc(sems[inc[0]], inc[1])
                    continue
                ins = fn(engh)
                ins.then_inc(sems[inc[0]], inc[1])

        with nc.Block() as block:
            @block.tensor
            def _(e):
                run(e, "pe")

            @block.scalar
            def _(e):
                run(e, "act")

            @block.vector
            def _(e):
                run(e, "dve")

            @block.gpsimd
            def _(e):
                run(e, "pool")

            @block.sync
            def _(e):
                run(e, "sp")


def build_nc(stop_after=None, debug=False):
    nc = bass.Bass("TRN2", target_bir_lowering=False)

    def din(name, shape):
        return nc.dram_tensor(name, list(shape), F32, kind="ExternalInput").ap()

    def dout(name, shape):
        return nc.dram_tensor(name, list(shape), F32, kind="ExternalOutput").ap()

    def dscr(name, shape, dt):
        if debug and name in ("mixT_d", "y0_d", "ys0_d", "mixs_d"):
            return nc.dram_tensor(name, list(shape), dt, kind="ExternalOutput").ap()
        return nc.dram_tensor(name, list(shape), dt).ap()

    x_p = din("x_p", [T, D])
    x_s = din("x_s", [1, D])
    ck = din("ck", [2, 2048, 1024])
    cv = din("cv", [2, 2048, 1024])
    st_pool = din("st_pool", [2, 15, 1024])
    st_conv = din("st_conv", [2, 3, 6144])
    st_delta = din("st_delta", [2, 16, 128, 128])
    norm_w = din("norm_w", [2, D])
    w_in = din("w_in", [2, D, DIN])
    cw_col = din("cw_col", [2, 128, 48, 4])
    conv_w = din("conv_w", [2, 4, 6144])
    a_log = din("a_log", [2, 16])
    dt_bias = din("dt_bias", [2, 16])
    dnw_col = din("dnw_col", [2, 128, 1])
    dnw_row = din("dnw_row", [2, 128])
    pool_w = din("pool_w", [2, 4, 256, 256])
    psc_col = din("psc_col", [2, 128, 8])
    psc_row = din("psc_row", [2, 1024])
    w_out = din("w_out", [2, D, D])
    fnw = din("fnw", [1, D])
    consts = din("consts", [128, NCONST])

    y_p = dout("y_p", [T, D])
    y_s = dout("y_s", [1, D])
    wk_p = dout("wk_p", [2, T, 1024])
    wv_p = dout("wv_p", [2, T, 1024])
    pool_p = dout("pool_p", [2, 15, 1024])
    conv_p = dout("conv_p", [2, 3, 6144])
    delta_p = dout("delta_p", [2, 16, 128, 128])
    wk_s = dout("wk_s", [2, 2048, 1024])
    wv_s = dout("wv_s", [2, 2048, 1024])
    pool_s = dout("pool_s", [2, 15, 1024])
    conv_s = dout("conv_s", [2, 3, 6144])
    delta_s = dout("delta_s", [2, 16, 128, 128])

    hT_d = dscr("hT_d", [KC, 128, T], BF16)
    qaT_d = dscr("qaT_d", [8, 128, T], BF16)
    kaT_d = dscr("kaT_d", [8, 128, T], BF16)
    gT_d = dscr("gT_d", [32, 128, T], BF16)
    u_d = dscr("u_d", [T, 1024], F32)
    qkvT_d = dscr("qkvT_d", [48, 128, T], F32)
    ab_d = dscr("ab_d", [T, 32], F32)
    mixT_d = dscr("mixT_d", [KC, 128, T], BF16)
    yl_d = [dscr("y0_d", [T, D], F32), dscr("y1_d", [T, D], F32)]
    zs_d = dscr("zs_d", [1, DIN], F32)
    mixs_d = dscr("mixs_d", [1, D], BF16)
    ys_d = [dscr("ys0_d", [1, D], F32), dscr("ys1_d", [1, D], F32)]

    with contextlib.ExitStack() as es:
        S = Sched(nc, es)
        ps = [es.enter_context(nc.psum_tensor("ps%d" % i, [128, 512], F32)) for i in range(8)]
        PR = [Res() for _ in range(8)]
        psb = [p.bitcast(BF16) for p in ps]
        rr = {"ps": 0}

        def nbank(lo=0, hi=8):
            b = lo + rr["ps"] % (hi - lo)
            rr["ps"] += 1
            return b

        def tile(ctx, name, shape, dt):
            rr["tn"] = rr.get("tn", 0) + 1
            return Tl(ctx.enter_context(nc.sbuf_tensor("%s_%d" % (name, rr["tn"]), list(shape), dt)))

        def MM(out, lhsT, rhs, st, sp, reads, writes):
            S.op("pe", lambda e: e.matmul(out, lhsT=lhsT, rhs=rhs, start=st, stop=sp), reads, writes)

        def TR(out, in_, ident, reads, writes):
            S.op("pe", lambda e: e.transpose(out=out, in_=in_, identity=ident), reads, writes)

        def ACT(out, in_, func, reads, writes, **kw):
            S.op("act", lambda e: e.activation(out=out, in_=in_, func=func, **kw), reads, writes)

        def CP(eng, out, in_, reads, writes):
            if eng == "act":
                S.op("act", lambda e: e.copy(out=out, in_=in_), reads, writes)
            else:
                S.op(eng, lambda e: e.tensor_copy(out=out, in_=in_), reads, writes)

        def TT(eng, out, in0, in1, op, reads, writes):
            S.op(eng, lambda e: e.tensor_tensor(out=out, in0=in0, in1=in1, op=op), reads, writes)

        def TS(eng, out, in0, s1, s2, op0, op1, reads, writes):
            if s2 is None:
                S.op(eng, lambda e: e.tensor_scalar(out=out, in0=in0, scalar1=s1, scalar2=None, op0=op0), reads, writes)
            else:
                S.op(eng, lambda e: e.tensor_scalar(out=out, in0=in0, scalar1=s1, scalar2=s2, op0=op0, op1=op1), reads, writes)

        def STT(eng, out, in0, scalar, in1, op0, op1, reads, writes):
            S.op(eng, lambda e: e.scalar_tensor_tensor(out=out, in0=in0, scalar=scalar, in1=in1, op0=op0, op1=op1), reads, writes)

        def RECIP(out, in_, reads, writes):
            S.op("dve", lambda e: e.reciprocal(out=out, in_=in_), reads, writes)

        def MSET(eng, ap, val, writes):
            S.op(eng, lambda e: e.memset(ap, val), (), writes)

        def RSUM(out, in_, reads, writes):
            S.op("dve", lambda e: e.reduce_sum(out=out, in_=in_, axis=AX.X), reads, writes)

        def DMA(q, out, in_, reads=(), writes=()):
            S.dma(q, lambda e: e.dma_start(out=out, in_=in_), reads, writes)

        def phase_end():
            S.barrier()
            S.flush()

        cst = tile(es, "cst", [128, 640], F32)
        identb = tile(es, "identb", [128, 128], BF16)
        onesb = tile(es, "onesb", [128, 128], BF16)
        hs_col = tile(es, "hs_col", [128, KC], BF16)
        DMA("sp", cst.t[:], consts[:, 0:640], (), [cst.r])
        CP("dve", identb.t[:], cst.t[:, C_ID:C_ID + 128], [cst.r], [identb.r])
        CP("dve", onesb.t[:], cst.t[:, C_ONES:C_ONES + 128], [cst.r], [onesb.r])
        identf = cst.t[:, C_ID:C_ID + 128]
        onesf = cst.t[:, C_ONES:C_ONES + 128]
        trif = cst.t[:, C_TRI:C_TRI + 128]
        mnI = cst.t[:, C_MNI:C_MNI + 128]
        mnS = cst.t[:, C_MNS:C_MNS + 128]

        def row2col(row_ap_fn, n, out_tile, one_ap, reads):
            b = nbank()
            for c in range(n):
                MM(ps[b][:, c:c + 1], row_ap_fn(c), one_ap, True, True, reads, [PR[b]])
            CP("dve", out_tile.t[:, 0:n], ps[b][:, 0:n], [PR[b]], [out_tile.r])

        def norm_phase(src, gain_row, hT_out, final_dst, xs_src, xs_dst):
            with contextlib.ExitStack() as ph:
                xt = [tile(ph, "xt%d" % i, [128, D], F32) for i in range(2)]
                gainb = tile(ph, "gainb", [128, D], F32)
                junk = tile(ph, "junk", [128, D], BF16)
                ss = tile(ph, "ss", [128, 40], F32)
                ssr = [Res() for _ in range(17)]
                DMA("sp", gainb.t[:], gain_row.partition_broadcast(128), (), [gainb.r])
                MSET("dve", ss.t[:], 0.0, ssr)
                if final_dst is None:
                    hb = [tile(ph, "hb%d" % i, [128, D], BF16) for i in range(2)]
                    hTt = [tile(ph, "hTt%d" % i, [128, KC, 512], BF16) for i in range(2)]
                else:
                    yo = [tile(ph, "yo%d" % i, [128, D], F32) for i in range(2)]
                for tt in range(17):
                    np_ = 128 if tt < 16 else 1
                    x = xt[tt % 2]
                    if tt < 16:
                        DMA("sp", x.t[:], src[tt * 128:(tt + 1) * 128, :], (), [x.r])
                    else:
                        DMA("sp", x.t[0:1, :], xs_src, (), [x.r])
                    ACT(junk.t[0:np_, :], x.t[0:np_, :], AF.Square, [x.r], [junk.r, ssr[tt]],
                        accum_out=ss.t[0:np_, tt:tt + 1])
                    ACT(ss.t[0:np_, 20 + tt:21 + tt], ss.t[0:np_, tt:tt + 1], AF.Sqrt, [ssr[tt]], [ssr[tt]],
                        scale=1.0 / D, bias=EPS)
                    RECIP(ss.t[0:np_, 20 + tt:21 + tt], ss.t[0:np_, 20 + tt:21 + tt], [ssr[tt]], [ssr[tt]])
                    rstd = ss.t[0:np_, 20 + tt:21 + tt]
                    if final_dst is None:
                        h = hb[tt % 2]
                        STT("dve", h.t[0:np_, :], x.t[0:np_, :], rstd, gainb.t[0:np_, :], ALU.mult, ALU.mult,
                            [x.r, ssr[tt], gainb.r], [h.r])
                        if tt == 16:
                            row2col(lambda c: h.t[0:1, c * 128:(c + 1) * 128], KC, hs_col, identb.t[0:1, 0:1],
                                    [h.r, identb.r])
                            continue
                        ht = hTt[(tt // 4) % 2]
                        for c4 in range(4):
                            b = nbank()
                            for j in range(8):
                                c = c4 * 8 + j
                                TR(psb[b][:, j * 128:(j + 1) * 128], h.t[:, c * 128:(c + 1) * 128], identb.t[:],
                                   [h.r, identb.r], [PR[b]])
                            CP("act" if c4 % 2 == 0 else "dve",
                               ht.t[:, c4 * 8:(c4 + 1) * 8, (tt % 4) * 128:(tt % 4 + 1) * 128],
                               psb[b][:, :].rearrange("p (j t) -> p j t", t=128), [PR[b]], [ht.r])
                        if tt % 4 == 3:
                            tb = tt // 4
                            DMA("pool", hT_out.rearrange("c p t -> p c t")[:, :, tb * 512:(tb + 1) * 512], ht.t[:],
                                [ht.r], ())
                    else:
                        y = yo[tt % 2]
                        STT("dve", y.t[0:np_, :], x.t[0:np_, :], rstd, gainb.t[0:np_, :], ALU.mult, ALU.mult,
                            [x.r, ssr[tt], gainb.r], [y.r])
                        if tt < 16:
                            DMA("pool", final_dst[tt * 128:(tt + 1) * 128, :], y.t[:], [y.r], ())
                        else:
                            DMA("sp", xs_dst, y.t[0:1, :], [y.r], ())
                phase_end()

        def proj_phase(lhs_d, W, groups, sample_dst, resid=None):
            with contextlib.ExitStack() as ph:
                A = tile(ph, "A", [128, KC, 1024], BF16)
                Ar = [Res() for _ in range(4)]
                Wb = [tile(ph, "Wb%d" % i, [128, KC, 512], BF16) for i in range(2)]
                Wr = [[Res() for _ in range(4)] for _ in range(2)]
                stF = [tile(ph, "stF%d" % i, [128, 512], F32) for i in range(4)]
                stB = [tile(ph, "stB%d" % i, [128, 512], BF16) for i in range(4)]
                xr = [tile(ph, "xr%d" % i, [128, 512], F32) for i in range(3)]
                srow = [tile(ph, "srow%d" % i, [1, 512], F32) for i in range(2)]
                cnt = {"f": 0, "b": 0, "x": 0, "s": 0, "w": 0}
                lhs_v = lhs_d.rearrange("c p t -> p c t")
                Wv = W.rearrange("(c p) n -> p c n", p=128)

                def nstF():
                    cnt["f"] += 1
                    return stF[cnt["f"] % 4]

                def nstB():
                    cnt["b"] += 1
                    return stB[cnt["b"] % 4]

                env = dict(nstF=nstF, nstB=nstB, xr=xr, cnt=cnt)
                for half in range(2):
                    for q in range(4):
                        DMA("sp", A.t[:, q * 8:(q + 1) * 8, :],
                            lhs_v[:, q * 8:(q + 1) * 8, half * 1024:(half + 1) * 1024], (), [Ar[q]])
                    for (col0, nco, sinks) in groups:
                        buf = cnt["w"] % 2
                        cnt["w"] += 1
                        w = Wb[buf]
                        for q in range(4):
                            DMA("pool", w.t[:, q * 8:(q + 1) * 8, 0:nco], Wv[:, q * 8:(q + 1) * 8, col0:col0 + nco],
                                (), [Wr[buf][q]])
                        for kind, fn in sinks:
                            if kind == "fm":
                                for fc in range(nco // 128):
                                    for tb in range(2):
                                        b = nbank(0, 6)
                                        for c in range(KC):
                                            MM(ps[b][:, 0:512], w.t[:, c, fc * 128:(fc + 1) * 128],
                                               A.t[:, c, tb * 512:(tb + 1) * 512], c == 0, c == KC - 1,
                                               [Wr[buf][c // 8], Ar[c // 8]], [PR[b]])
                                        fn(env, b, fc, half * 1024 + tb * 512)
                            else:
                                for tt in range(8):
                                    if kind == "tml" and not (half == 1 and tt == 7):
                                        continue
                                    ttg = half * 8 + tt
                                    pre = fn(env, None, ttg, col0, nco) if kind == "tmr" else None
                                    b = nbank(0, 6)
                                    for c in range(KC):
                                        MM(ps[b][:, 0:nco], A.t[:, c, tt * 128:(tt + 1) * 128], w.t[:, c, 0:nco],
                                           c == 0, c == KC - 1, [Wr[buf][c // 8], Ar[c // 8]], [PR[b]])
                                    if kind == "tmr":
                                        fn(env, b, ttg, col0, nco, pre)
                                    else:
                                        fn(env, b, ttg, col0, nco)
                        if half == 0 and sample_dst is not None:
                            b = 6 + cnt["s"] % 2
                            sr = srow[cnt["s"] % 2]
                            cnt["s"] += 1
                            for c in range(KC):
                                MM(ps[b][0:1, 0:nco], hs_col.t[:, c:c + 1], w.t[:, c, 0:nco], c == 0, c == KC - 1,
                                   [Wr[buf][c // 8], hs_col.r], [PR[b]])
                            sample_dst(env, b, sr, col0, nco)
                phase_end()

        def sink_fm_store(dest_d, chunk0, dt, gate=False):
            def fn(env, b, fc, tok0):
                st = env["nstB"]() if dt == BF16 else env["nstF"]()
                if gate:
                    ACT(st.t[:], ps[b][:, 0:512], AF.Silu, [PR[b]], [st.r])
                else:
                    CP("dve", st.t[:], ps[b][:, 0:512], [PR[b]], [st.r])
                DMA("sp", dest_d[chunk0 + fc][:, tok0:tok0 + 512], st.t[:], [st.r], ())
            return ("fm", fn)

        def sink_tm_store(dest2d, dcol0, eng="act"):
            def fn(env, b, ttg, col0, nco):
                st = env["nstF"]()
                CP(eng, st.t[:, 0:nco], ps[b][:, 0:nco], [PR[b]], [st.r])
                DMA("sp", dest2d[ttg * 128:(ttg + 1) * 128, dcol0:dcol0 + nco], st.t[:, 0:nco], [st.r], ())
            return ("tm", fn)

        def sink_tml_rows(dest_rows, dcol0):
            def fn(env, b, ttg, col0, nco):
                st = env["nstF"]()
                CP("act", st.t[:, 0:nco], ps[b][:, 0:nco], [PR[b]], [st.r])
                DMA("sp", dest_rows[:, dcol0:dcol0 + nco], st.t[125:128, 0:nco], [st.r], ())
            return ("tml", fn)

        def sink_tm_resid(x_src, y_dst):
            def fn(env, b, ttg, col0, nco, pre=None):
                if b is None:
                    env["cnt"]["x"] += 1
                    xx = env["xr"][env["cnt"]["x"] % 3]
                    DMA("sp", xx.t[:, 0:nco], x_src[ttg * 128:(ttg + 1) * 128, col0:col0 + nco], (), [xx.r])
                    return xx
                st = env["nstF"]()
                TT("dve", st.t[:, 0:nco], ps[b][:, 0:nco], pre.t[:, 0:nco], ALU.add, [PR[b], pre.r], [st.r])
                DMA("sp", y_dst[ttg * 128:(ttg + 1) * 128, col0:col0 + nco], st.t[:, 0:nco], [st.r], ())
            return ("tmr", fn)

        def attn_phase(l):
            with contextlib.ExitStack() as ph:
                maskb = tile(ph, "maskb", [128, 19 * 128], BF16)
                DMA("pool", maskb.t[:], consts[:, C_MASK:C_MASK + 19 * 128], (), [maskb.r])
                qT = [tile(ph, "qT%d" % i, [128, T], BF16) for i in range(2)]
                kT = [tile(ph, "kT%d" % i, [128, T], BF16) for i in range(2)]
                V = [tile(ph, "V%d" % i, [128, 16, 128], BF16) for i in range(2)]
                gT = [tile(ph, "gT%d" % i, [128, T], BF16) for i in range(2)]
                E = [tile(ph, "E%d" % i, [128, 512], BF16) for i in range(3)]
                P = [tile(ph, "P%d" % i, [128, 512], BF16) for i in range(3)]
                osb = [tile(ph, "osb%d" % i, [128, 512], F32) for i in range(2)]
                rd = [tile(ph, "rd%d" % i, [128, 512], F32) for i in range(2)]
                mx = [tile(ph, "mx%d" % i, [128, 512], BF16) for i in range(2)]
                wvv = wv_p[l].rearrange("(tt p) c -> p tt c", p=128)
                it = 0
                for h in range(8):
                    bf = h % 2
                    DMA("sp", qT[bf].t[:], qaT_d[h], (), [qT[bf].r])
                    DMA("sp", kT[bf].t[:], kaT_d[h], (), [kT[bf].r])
                    DMA("sp", gT[bf].t[:], gT_d[h], (), [gT[bf].r])
                    DMA("pool", V[bf].t[:], wvv[:, :, h * 128:(h + 1) * 128], (), [V[bf].r])
                    its = [(qg, kt) for qg in range(4) for kt in range(4 * qg + 4)]

                    def emit_st(n):
                        qg_, kt_ = its[n]
                        sbk_ = (it + n) % 4
                        MM(ps[sbk_][:, 0:512], kT[bf].t[:, kt_ * 128:(kt_ + 1) * 128],
                           qT[bf].t[:, qg_ * 512:(qg_ + 1) * 512], True, True, [kT[bf].r, qT[bf].r], [PR[sbk_]])
                    emit_st(0)
                    emit_st(1)
                    for n, (qg, kt) in enumerate(its):
                        ob = 4 + qg % 2
                        db = 6 + qg % 2
                        nk = 4 * qg + 4
                        if True:
                            sbk = (it + n) % 4
                            e = E[(it + n) % 3]
                            p = P[(it + n) % 3]
                            if n + 2 < len(its):
                                emit_st(n + 2)
                            ACT(e.t[:], ps[sbk][:, 0:512], AF.Exp, [PR[sbk]], [e.r], scale=SCALE_A)
                            d0 = 4 * qg - kt + 3
                            TT("dve", p.t[:], e.t[:], maskb.t[:, d0 * 128:(d0 + 4) * 128], ALU.mult,
                               [e.r, maskb.r], [p.r])
                            MM(ps[ob][:, 0:512], V[bf].t[:, kt, :], p.t[:], kt == 0, kt == nk - 1,
                               [V[bf].r, p.r], [PR[ob]])
                            MM(ps[db][:, 0:512], onesb.t[:], p.t[:], kt == 0, kt == nk - 1,
                               [onesb.r, p.r], [PR[db]])
                        if kt != nk - 1:
                            continue
                        r = rd[qg % 2]
                        o = osb[qg % 2]
                        m = mx[qg % 2]
                        RECIP(r.t[:], ps[db][:, 0:512], [PR[db]], [r.r])
                        TT("dve", o.t[:], ps[ob][:, 0:512], r.t[:], ALU.mult, [PR[ob], r.r], [o.r])
                        TT("pool", m.t[:], o.t[:], gT[bf].t[:, qg * 512:(qg + 1) * 512], ALU.mult,
                           [o.r, gT[bf].r], [m.r])
                        DMA("sp", mixT_d[h][:, qg * 512:(qg + 1) * 512], m.t[:], [m.r], ())
                    it += len(its)
                phase_end()

        def pool_phase(l):
            with contextlib.ExitStack() as ph:
                u = tile(ph, "u", [128, 16, 1024], BF16)
                dm = tile(ph, "dm", [128, 12, 128], BF16)
                pw = tile(ph, "pw", [128, 4, 2, 256], BF16)
                psc = tile(ph, "psc", [128, 8], F32)
                gtB = tile(ph, "gtB", [128, 8, T], BF16)
                ym = [[tile(ph, "ym%d_%d" % (cc, k), [128, 512], BF16) for k in range(2)] for cc in range(2)]
                mx = [tile(ph, "pmx%d" % i, [128, 512], BF16) for i in range(3)]
                DMA("pool", u.t[:], u_d.rearrange("(tt p) c -> p tt c", p=128), (), [u.r])
                DMA("pool", dm.t[:], consts[:, C_DM:C_DM + 12 * 128].rearrange("p (a b) -> p a b", b=128), (), [dm.r])
                DMA("pool", pw.t[:], pool_w[l].rearrange("g (cc p) d -> p g cc d", p=128), (), [pw.r])
                DMA("sp", psc.t[:], psc_col[l], (), [psc.r])
                DMA("sp", gtB.t[:], gT_d[8:16].rearrange("c p t -> p c t"), (), [gtB.r])
                it = 0
                for g in range(4):
                    for tb in range(4):
                        for cc in range(2):
                            b = nbank(0, 4)
                            for t4 in range(4):
                                tt = tb * 4 + t4
                                cs = slice(g * 256 + cc * 128, g * 256 + cc * 128 + 128)
                                MM(ps[b][:, t4 * 128:(t4 + 1) * 128], u.t[:, tt, cs],
                                   dm.t[:, g * 3 + (2 if tt == 0 else 0), :], True, tt == 0, [u.r, dm.r], [PR[b]])
                                if tt > 0:
                                    MM(ps[b][:, t4 * 128:(t4 + 1) * 128], u.t[:, tt - 1, cs], dm.t[:, g * 3 + 1, :],
                                       False, True, [u.r, dm.r], [PR[b]])
                            CP("act", ym[cc][it % 2].t[:], ps[b][:, 0:512], [PR[b]], [ym[cc][it % 2].r])
                        for dd in range(2):
                            b2 = 4 + nbank(0, 4)
                            for cc in range(2):
                                MM(ps[b2][:, 0:512], pw.t[:, g, cc, dd * 128:(dd + 1) * 128], ym[cc][it % 2].t[:],
                                   cc == 0, cc == 1, [pw.r, ym[cc][it % 2].r], [PR[b2]])
                            ch = g * 2 + dd
                            m = mx[(it * 2 + dd) % 3]
                            STT("dve", m.t[:], ps[b2][:, 0:512], psc.t[:, ch:ch + 1], gtB.t[:, ch, tb * 512:(tb + 1) * 512],
                                ALU.mult, ALU.mult, [PR[b2], psc.r, gtB.r], [m.r])
                            DMA("sp", mixT_d[8 + ch][:, tb * 512:(tb + 1) * 512], m.t[:], [m.r], ())
                        it += 1
                phase_end()

        def gdn_phase(l):
            with contextlib.ExitStack() as ph:
                cwc = tile(ph, "cwc", [128, 48, 4], F32)
                dnw = tile(ph, "dnw", [128, 1], F32)
                DMA("sp", cwc.t[:], cw_col[l], (), [cwc.r])
                DMA("sp", dnw.t[:], dnw_col[l], (), [dnw.r])
                ab = tile(ph, "ab", [128, 16, 32], F32)
                alb = tile(ph, "alb", [128, 16], F32)
                dtb = tile(ph, "dtb", [128, 16], F32)
                DMA("sp", ab.t[:], ab_d.rearrange("(tt p) c -> p tt c", p=128), (), [ab.r])
                DMA("sp", alb.t[:], a_log[l:l + 1, :].partition_broadcast(128), (), [alb.r])
                DMA("sp", dtb.t[:], dt_bias[l:l + 1, :].partition_broadcast(128), (), [dtb.r])

                def g3(name):
                    return tile(ph, name, [128, 16, 16], F32)
                gg, beta, lnb, gcs, gl, eg, kdw, bw, egl, gbs, tmp = [g3(n) for n in
                    ("gg", "beta", "lnb", "gcs", "gl", "eg", "kdw", "bw", "egl", "gbs", "tmpg")]

                def bc_h(tl):
                    return bass.AP(tl.t, 0, [[16, 128], [0, 16], [1, 16]])
                ACT(alb.t[:], alb.t[:], AF.Exp, [alb.r], [alb.r])
                TT("dve", tmp.t[:], ab.t[:, :, 0:16], bc_h(dtb), ALU.add, [ab.r, dtb.r], [tmp.r])
                ACT(tmp.t[:], tmp.t[:], AF.Exp, [tmp.r], [tmp.r])
                ACT(tmp.t[:], tmp.t[:], AF.Ln, [tmp.r], [tmp.r], bias=1.0)
                STT("dve", gg.t[:], tmp.t[:], -1.0, bc_h(alb), ALU.mult, ALU.mult, [tmp.r, alb.r], [gg.r])
                ACT(beta.t[:], ab.t[:, :, 16:32], AF.Sigmoid, [ab.r], [beta.r])
                ACT(lnb.t[:], beta.t[:], AF.Ln, [beta.r], [lnb.r])
                b = nbank()
                MM(ps[b][:, 0:256], trif, gg.t[:].rearrange("p a b -> p (a b)"), True, True, [cst.r, gg.r], [PR[b]])
                CP("dve", gcs.t[:].rearrange("p a b -> p (a b)"), ps[b][:, 0:256], [PR[b]], [gcs.r])
                b = nbank()
                MM(ps[b][:, 0:256], onesf, gg.t[:].rearrange("p a b -> p (a b)"), True, True, [cst.r, gg.r], [PR[b]])
                CP("dve", gl.t[:].rearrange("p a b -> p (a b)"), ps[b][:, 0:256], [PR[b]], [gl.r])
                ACT(eg.t[:], gcs.t[:], AF.Exp, [gcs.r], [eg.r])
                TT("dve", tmp.t[:], gl.t[:], gcs.t[:], ALU.subtract, [gl.r, gcs.r], [tmp.r])
                ACT(kdw.t[:], tmp.t[:], AF.Exp, [tmp.r], [kdw.r])
                TT("dve", bw.t[:], beta.t[:], eg.t[:], ALU.mult, [beta.r, eg.r], [bw.r])
                ACT(egl.t[:], gl.t[:], AF.Exp, [gl.r], [egl.r])
                TT("dve", gbs.t[:], gcs.t[:], lnb.t[:], ALU.add, [gcs.r, lnb.r], [gbs.r])

                def bc_d(tl, h0, nh, c=None):
                    if c is None:
                        return bass.AP(tl.t, h0, [[256, 128], [16, 16], [0, 128]])
                    return bass.AP(tl.t, c * 16 + h0, [[256, 128], [1, nh], [0, 128]])

                for hg in range(4):
                    h0 = hg * 4
                    with contextlib.ExitStack() as hp:
                        qn = [tile(hp, "qn%d" % i, [128, T], BF16) for i in range(4)]
                        kn = [tile(hp, "kn%d" % i, [128, T], BF16) for i in range(4)]
                        kbg = [tile(hp, "kbg%d" % i, [128, 16, 128], BF16) for i in range(4)]
                        kdec = [tile(hp, "kdec%d" % i, [128, 16, 128], BF16) for i in range(4)]
                        vb = [tile(hp, "vb%d" % i, [128, 16, 128], BF16) for i in range(4)]
                        gC = tile(hp, "gC", [128, 4, T], BF16)
                        DMA("sp", gC.t[:], gT_d[16 + h0:16 + h0 + 4].rearrange("c p t -> p c t"), (), [gC.r])
                        with contextlib.ExitStack() as c1:
                            X = [tile(c1, "X%d" % i, [128, T + 4], BF16) for i in range(3)]
                            Xo = [tile(c1, "Xo%d" % i, [128, T + 4], BF16) for i in range(3)]
                            dg = [tile(c1, "dg%d" % i, [128, 4, 128], BF16) for i in range(3)]
                            yq = tile(c1, "yq", [128, T], F32)
                            sqb = tile(c1, "sqb", [128, T], BF16)
                            vT = tile(c1, "vT", [128, T], BF16)
                            rt = [tile(c1, "rt%d" % i, [128, 512], F32) for i in range(2)]
                            ktm = tile(c1, "ktm", [128, 16, 128], BF16)
                            for xx in X:
                                MSET("dve", xx.t[:, 0:3], 0.0, [xx.r])
                            for xx in Xo:
                                MSET("dve", xx.t[:, 0:4], 0.0, [xx.r])
                            idf4c = bass.AP(cst.t, C_ID, [[640, 128], [0, 4], [1, 128]])
                            k = 0
                            for hh in range(4):
                                h = h0 + hh
                                for fam in range(3):
                                    ch = fam * 16 + h
                                    xx = X[k % 3]
                                    xo = Xo[k % 3]
                                    d_ = dg[k % 3]
                                    k += 1
                                    DMA("pool", xx.t[:, 3:T + 3], qkvT_d[ch], (), [xx.r])
                                    DMA("pool", xo.t[:, 4:T + 4], qkvT_d[ch], (), [xo.r])
                                    TT("dve", d_.t[:], idf4c, bass.AP(cwc.t, ch * 4, [[192, 128], [1, 4], [0, 128]]), ALU.mult,
                                       [cst.r, cwc.r], [d_.r])
                                    for tb in range(4):
                                        b = nbank()
                                        for i in range(4):
                                            if i % 2 == 0:
                                                rhs_ = xx.t[:, tb * 512 + i:tb * 512 + i + 512]
                                            else:
                                                rhs_ = xo.t[:, tb * 512 + i + 1:tb * 512 + i + 1 + 512]
                                            MM(ps[b][:, 0:512], d_.t[:, i, :], rhs_,
                                               i == 0, i == 3, [d_.r, xx.r, xo.r], [PR[b]])
                                        if fam == 2:
                                            ACT(vT.t[:, tb * 512:(tb + 1) * 512], ps[b][:, 0:512], AF.Silu, [PR[b]], [vT.r])
                                        else:
                                            ACT(yq.t[:, tb * 512:(tb + 1) * 512], ps[b][:, 0:512], AF.Silu, [PR[b]], [yq.r])
                                    if fam == 2:
                                        src_t, dsts = vT, [(vb[hh], beta)]
                                    else:
                                        ACT(sqb.t[:], yq.t[:], AF.Square, [yq.r], [sqb.r])
                                        dst = qn[hh] if fam == 0 else kn[hh]
                                        for tb in range(4):
                                            b = nbank()
                                            r_ = rt[tb % 2]
                                            MM(ps[b][:, 0:512], onesb.t[:], sqb.t[:, tb * 512:(tb + 1) * 512], True, True,
                                               [onesb.r, sqb.r], [PR[b]])
                                            ACT(r_.t[:], ps[b][:, 0:512], AF.Ln, [PR[b]], [r_.r], bias=EPS)
                                            ACT(r_.t[:], r_.t[:], AF.Exp, [r_.r], [r_.r], scale=-0.5)
                                            STT("dve", dst.t[:, tb * 512:(tb + 1) * 512], yq.t[:, tb * 512:(tb + 1) * 512],
                                                (SCALE_A if fam == 0 else 1.0), r_.t[:], ALU.mult, ALU.mult,
                                                [yq.r, r_.r], [dst.r])
                                        if fam == 0:
                                            continue
                                        src_t, dsts = kn[hh], [(kbg[hh], bw), (kdec[hh], kdw)]
                                    for half in range(2):
                                        b = nbank()
                                        for j in range(8):
                                            tt = half * 8 + j
                                            TR(psb[b][:, j * 128:(j + 1) * 128], src_t.t[:, tt * 128:(tt + 1) * 128],
                                               identb.t[:], [src_t.r, identb.r], [PR[b]])
                                        CP("act", ktm.t[:, half * 8:(half + 1) * 8, :],
                                           psb[b][:, :].rearrange("p (j t) -> p j t", t=128), [PR[b]], [ktm.r])
                                    for dtile, sc in dsts:
                                        TT("dve", dtile.t[:], ktm.t[:], bc_d(sc, h, 1), ALU.mult, [ktm.r, sc.r], [dtile.r])
                            phase_end()
                        with contextlib.ExitStack() as sc_:
                            def t512(name, dt, n=2):
                                return [tile(sc_, "%s%d" % (name, i), [128, 4, 128], dt) for i in range(n)]
                            Rg = t512("Rg", F32)
                            Rb = t512("Rb", F32)
                            t1 = t512("t1", F32)
                            t2 = t512("t2", F32)
                            E1 = t512("E1", F32)
                            E2 = t512("E2", F32)
                            egr = t512("egr", BF16)
                            qg_ = t512("qg", BF16)
                            qkd = t512("qkd", BF16)
                            NmS = [t512("NmA", BF16, 3), t512("NmB", BF16, 3)]
                            MmS = [t512("MmA", BF16, 3), t512("MmB", BF16, 3)]
                            RmS = [t512("RmA", BF16, 3), t512("RmB", BF16, 3)]
                            usb = t512("usb", F32)
                            wT = t512("wT", BF16)
                            vnew = t512("vnew", BF16)
                            sq_ = t512("sq", BF16)
                            rt_ = t512("rtn", F32)
                            om = t512("om", F32)
                            mxb = t512("mxb", BF16)
                            Sf = tile(sc_, "Sf", [128, 4, 128], F32)
                            Sb = [tile(sc_, "Sb%d" % i, [128, 4, 128], BF16) for i in range(2)]
                            MSET("dve", Sf.t[:], 0.0, [Sf.r])
                            MSET("dve", Sb[0].t[:], 0.0, [Sb[0].r])
                            idf4 = bass.AP(cst.t, C_ID, [[640, 128], [0, 4], [1, 128]])
                            idb4 = bass.AP(identb.t, 0, [[128, 128], [0, 4], [1, 128]])
                            mnI4 = bass.AP(cst.t, C_MNI, [[640, 128], [0, 4], [1, 128]])
                            mnS4 = bass.AP(cst.t, C_MNS, [[640, 128], [0, 4], [1, 128]])

                            def f2(t_):
                                return t_.t[:].rearrange("p a b -> p (a b)")
                            def prep(c):
                                i2 = c % 2
                                cs = slice(c * 128, (c + 1) * 128)
                                Nmm, Mmm, Rmm = NmS[i2], MmS[i2], RmS[i2]
                                TT("pool", Rg[i2].t[:], idf4, bc_d(gcs, h0, 4, c), ALU.mult, [cst.r, gcs.r], [Rg[i2].r])
                                TT("pool", Rb[i2].t[:], idf4, bc_d(gbs, h0, 4, c), ALU.mult, [cst.r, gbs.r], [Rb[i2].r])
                                bg = nbank()
                                MM(ps[bg][:, 0:512], onesf, f2(Rg[i2]), True, True, [cst.r, Rg[i2].r], [PR[bg]])
                                bb = nbank()
                                MM(ps[bb][:, 0:512], onesf, f2(Rb[i2]), True, True, [cst.r, Rb[i2].r], [PR[bb]])
                                yield
                                pg3 = ps[bg][:, 0:512].rearrange("p (a b) -> p a b", b=128)
                                pb3 = ps[bb][:, 0:512].rearrange("p (a b) -> p a b", b=128)
                                TT("dve", t1[i2].t[:], pg3, bc_d(gcs, h0, 4, c), ALU.subtract, [PR[bg], gcs.r], [t1[i2].r])
                                TT("dve", t1[i2].t[:], t1[i2].t[:], mnI4, ALU.add, [t1[i2].r, cst.r], [t1[i2].r])
                                ACT(E1[i2].t[:], t1[i2].t[:], AF.Exp, [t1[i2].r], [E1[i2].r])
                                TT("dve", t2[i2].t[:], pb3, bc_d(gcs, h0, 4, c), ALU.subtract, [PR[bb], gcs.r], [t2[i2].r])
                                TT("dve", t2[i2].t[:], t2[i2].t[:], mnS4, ALU.add, [t2[i2].r, cst.r], [t2[i2].r])
                                ACT(E2[i2].t[:], t2[i2].t[:], AF.Exp, [t2[i2].r], [E2[i2].r])
                                ACT(egr[i2].t[:], pg3, AF.Exp, [PR[bg]], [egr[i2].r])
                                for hh in range(4):
                                    TT("pool", qg_[i2].t[:, hh, :], qn[hh].t[:, cs], egr[i2].t[:, hh, :], ALU.mult,
                                       [qn[hh].r, egr[i2].r], [qg_[i2].r])
                                bG = nbank()
                                bQ = nbank()
                                for hh in range(4):
                                    MM(ps[bG][:, hh * 128:(hh + 1) * 128], kn[hh].t[:, cs], kn[hh].t[:, cs], True, True,
                                       [kn[hh].r], [PR[bG]])
                                    MM(ps[bQ][:, hh * 128:(hh + 1) * 128], kn[hh].t[:, cs], qn[hh].t[:, cs], True, True,
                                       [kn[hh].r, qn[hh].r], [PR[bQ]])
                                yield
                                N = Nmm[0]
                                M = Mmm[0]
                                R = Rmm[0]
                                STT("dve", f2(N), ps[bG][:, 0:512], -1.0, f2(E2[i2]), ALU.mult, ALU.mult,
                                    [PR[bG], E2[i2].r], [N.r])
                                TT("dve", f2(qkd[i2]), ps[bQ][:, 0:512], f2(E1[i2]), ALU.mult, [PR[bQ], E1[i2].r], [qkd[i2].r])
                                bT = nbank()
                                for hh in range(4):
                                    TR(psb[bT][:, hh * 128:(hh + 1) * 128], N.t[:, hh, :], identb.t[:], [N.r, identb.r], [PR[bT]])
                                yield
                                CP("act", f2(M), psb[bT][:, 0:512], [PR[bT]], [M.r])
                                TT("pool", R.t[:], N.t[:], idb4, ALU.add, [N.r, identb.r], [R.r])
                                cur = 0
                                for k in range(1, 7):
                                    nxt = (cur + 1) % 3
                                    N2, M2, R2 = Nmm[nxt], Mmm[nxt], Rmm[nxt]
                                    bM = nbank()
                                    for hh in range(4):
                                        MM(ps[bM][:, hh * 128:(hh + 1) * 128], N.t[:, hh, :], M.t[:, hh, :], True, True,
                                           [N.r, M.r], [PR[bM]])
                                    if k < 6:
                                        bN = nbank()
                                        for hh in range(4):
                                            MM(ps[bN][:, hh * 128:(hh + 1) * 128], M.t[:, hh, :], N.t[:, hh, :], True, True,
                                               [N.r, M.r], [PR[bN]])
                                    yield
                                    CP("act", f2(M2), ps[bM][:, 0:512], [PR[bM]], [M2.r])
                                    if k < 6:
                                        CP("dve", f2(N2), ps[bN][:, 0:512], [PR[bN]], [N2.r])
                                    bR = nbank()
                                    for hh in range(4):
                                        MM(ps[bR][:, hh * 128:(hh + 1) * 128], M2.t[:, hh, :], R.t[:, hh, :], True, True,
                                           [M2.r, R.r], [PR[bR]])
                                    yield
                                    TT("dve", f2(R2), ps[bR][:, 0:512], f2(R), ALU.add, [PR[bR], R.r], [R2.r])
                                    N, M, R = N2, M2, R2
                                    cur = nxt
                                bU = nbank()
                                bW = nbank()
                                for hh in range(4):
                                    MM(ps[bU][:, hh * 128:(hh + 1) * 128], R.t[:, hh, :], vb[hh].t[:, c, :], True, True,
                                       [R.r, vb[hh].r], [PR[bU]])
                                    MM(ps[bW][:, hh * 128:(hh + 1) * 128], kbg[hh].t[:, c, :], R.t[:, hh, :], True, True,
                                       [R.r, kbg[hh].r], [PR[bW]])
                                yield
                                CP("act", f2(usb[i2]), ps[bU][:, 0:512], [PR[bU]], [usb[i2].r])
                                CP("dve", f2(wT[i2]), ps[bW][:, 0:512], [PR[bW]], [wT[i2].r])

                            def scan_step(c):
                                i2 = c % 2
                                cs = slice(c * 128, (c + 1) * 128)
                                Sc = Sb[c % 2]
                                Sn = Sb[(c + 1) % 2]
                                bS = nbank()
                                for hh in range(4):
                                    MM(ps[bS][:, hh * 128:(hh + 1) * 128], wT[i2].t[:, hh, :], Sc.t[:, hh, :], True, True,
                                       [wT[i2].r, Sc.r], [PR[bS]])
                                TT("dve", f2(vnew[i2]), f2(usb[i2]), ps[bS][:, 0:512], ALU.subtract, [usb[i2].r, PR[bS]],
                                   [vnew[i2].r])
                                bO = nbank()
                                bD = nbank()
                                for hh in range(4):
                                    MM(ps[bO][:, hh * 128:(hh + 1) * 128], Sc.t[:, hh, :], qg_[i2].t[:, hh, :], True, False,
                                       [Sc.r, qg_[i2].r], [PR[bO]])
                                    MM(ps[bO][:, hh * 128:(hh + 1) * 128], vnew[i2].t[:, hh, :], qkd[i2].t[:, hh, :], False, True,
                                       [vnew[i2].r, qkd[i2].r], [PR[bO]])
                                    MM(ps[bD][:, hh * 128:(hh + 1) * 128], kdec[hh].t[:, c, :], vnew[i2].t[:, hh, :], True, True,
                                       [kdec[hh].r, vnew[i2].r], [PR[bD]])
                                TT("pool", Sf.t[:], Sf.t[:], bc_d(egl, h0, 4, c), ALU.mult, [Sf.r, egl.r], [Sf.r])
                                TT("dve", f2(Sf), f2(Sf), ps[bD][:, 0:512], ALU.add, [Sf.r, PR[bD]], [Sf.r])
                                CP("act", Sn.t[:], Sf.t[:], [Sf.r], [Sn.r])
                                ACT(f2(sq_[i2]), ps[bO][:, 0:512], AF.Square, [PR[bO]], [sq_[i2].r])
                                bq = nbank()
                                MM(ps[bq][:, 0:512], onesb.t[:], f2(sq_[i2]), True, True, [onesb.r, sq_[i2].r], [PR[bq]])
                                ACT(f2(rt_[i2]), ps[bq][:, 0:512], AF.Sqrt, [PR[bq]], [rt_[i2].r], scale=1.0 / 128, bias=EPS)
                                RECIP(f2(rt_[i2]), f2(rt_[i2]), [rt_[i2].r], [rt_[i2].r])
                                STT("dve", f2(om[i2]), ps[bO][:, 0:512], dnw.t[:, 0:1], f2(rt_[i2]), ALU.mult, ALU.mult,
                                    [PR[bO], dnw.r, rt_[i2].r], [om[i2].r])
                                TT("pool", mxb[i2].t[:], om[i2].t[:], gC.t[:, :, cs], ALU.mult, [om[i2].r, gC.r], [mxb[i2].r])
                                DMA("sp", mixT_d[16 + h0:16 + h0 + 4].rearrange("h p t -> p h t")[:, :, cs], mxb[i2].t[:],
                                    [mxb[i2].r], ())

                            for cp_ in range(8):
                                gens = [prep(2 * cp_), prep(2 * cp_ + 1)]
                                alive = [True, True]
                                while any(alive):
                                    for gi_ in range(2):
                                        if alive[gi_]:
                                            try:
                                                next(gens[gi_])
                                            except StopIteration:
                                                alive[gi_] = False
                                scan_step(2 * cp_)
                                scan_step(2 * cp_ + 1)
                            DMA("sp", delta_p[l, h0:h0 + 4].rearrange("h k v -> k h v"), Sf.t[:], [Sf.r], ())
                            phase_end()
                phase_end()

        def sample_phase(l):
            with contextlib.ExitStack() as ph:
                zs = tile(ph, "zs", [1, DIN], F32)
                DMA("sp", zs.t[:], zs_d, (), [zs.r])
                mixr = tile(ph, "mixr", [1, D], BF16)
                one1 = cst.t[0:1, C_ONES:C_ONES + 1]
                ones_row = cst.t[0:1, C_ONES:C_ONES + 128]
                DMA("sp", wk_s[l, 2047:2048, :], zs.t[0:1, 1024:2048], [zs.r], ())
                DMA("sp", wv_s[l, 2047:2048, :], zs.t[0:1, 2048:3072], [zs.r], ())
                DMA("sp", wk_s[l, 0:2047, :], ck[l, 1:2048, :], (), ())
                DMA("sp", wv_s[l, 0:2047, :], cv[l, 1:2048, :], (), ())
                DMA("sp", pool_s[l, 0:14, :], st_pool[l, 1:15, :], (), ())
                DMA("sp", pool_s[l, 14:15, :], zs.t[0:1, 4096:5120], [zs.r], ())
                DMA("sp", conv_s[l, 0:2, :], st_conv[l, 1:3, :], (), ())
                DMA("sp", conv_s[l, 2:3, :], zs.t[0:1, 6144:12288], [zs.r], ())
                with contextlib.ExitStack() as a_:
                    qb = tile(a_, "qb", [128, 1024], F32)
                    Kp = [tile(a_, "Kp%d" % i, [128, 1024], F32) for i in range(3)]
                    Vp = [tile(a_, "Vp%d" % i, [128, 1024], F32) for i in range(3)]
                    prod = tile(a_, "prod", [128, 1024], F32)
                    sc = tile(a_, "sc", [128, 32], F32)
                    num = tile(a_, "num", [8, 1024], F32)
                    den = tile(a_, "den", [8, 2], F32)
                    oar = tile(a_, "oar", [1, 1024], F32)
                    for hb_ in range(2):
                        b = nbank()
                        MM(ps[b][:, 0:512], ones_row, zs.t[0:1, hb_ * 512:(hb_ + 1) * 512], True, True, [cst.r, zs.r], [PR[b]])
                        CP("dve", qb.t[:, hb_ * 512:(hb_ + 1) * 512], ps[b][:, 0:512], [PR[b]], [qb.r])
                    MSET("dve", sc.t[:], 0.0, [sc.r])
                    for p_, d_ in enumerate((1, 4, 16)):
                        r0 = 2048 - 128 * d_
                        ksrc = ck[l, r0:2048, :].rearrange("(m d) c -> m d c", d=d_)[:, 0, :]
                        vsrc = cv[l, r0:2048, :].rearrange("(m d) c -> m d c", d=d_)[:, 0, :]
                        DMA("sp", Kp[p_].t[:], ksrc, (), [Kp[p_].r])
                        DMA("sp", Vp[p_].t[:], vsrc, (), [Vp[p_].r])
                        TT("dve", prod.t[:], Kp[p_].t[:], qb.t[:], ALU.mult, [Kp[p_].r, qb.r], [prod.r])
                        RSUM(sc.t[:, p_ * 8:(p_ + 1) * 8], prod.t[:].rearrange("p (h d) -> p h d", d=128), [prod.r], [sc.r])
                    TT("dve", prod.t[0:1, :], zs.t[0:1, 1024:2048], zs.t[0:1, 0:1024], ALU.mult, [zs.r], [prod.r])
                    RSUM(sc.t[0:1, 24:32], prod.t[0:1, :].rearrange("p (h d) -> p h d", d=128), [prod.r], [sc.r])
                    ACT(sc.t[:, 0:24], sc.t[:, 0:24], AF.Exp, [sc.r], [sc.r], scale=SCALE_A)
                    ACT(sc.t[0:1, 24:32], sc.t[0:1, 24:32], AF.Exp, [sc.r], [sc.r], scale=SCALE_A)
                    TS("dve", sc.t[0:1, 24:32], sc.t[0:1, 24:32], 3.0, None, ALU.mult, None, [sc.r], [sc.r])
                    b0 = nbank()
                    b1 = nbank()
                    bd = nbank()
                    for hb_, bk in ((0, b0), (1, b1)):
                        for p_ in range(3):
                            MM(ps[bk][0:8, 0:512], sc.t[:, p_ * 8:(p_ + 1) * 8], Vp[p_].t[:, hb_ * 512:(hb_ + 1) * 512],
                               p_ == 0, False, [sc.r, Vp[p_].r], [PR[bk]])
                        MM(ps[bk][0:8, 0:512], sc.t[0:1, 24:32], zs.t[0:1, 2048 + hb_ * 512:2048 + (hb_ + 1) * 512],
                           False, True, [sc.r, zs.r], [PR[bk]])
                        CP("dve", num.t[:, hb_ * 512:(hb_ + 1) * 512], ps[bk][0:8, 0:512], [PR[bk]], [num.r])
                    for p_ in range(3):
                        MM(ps[bd][0:8, 0:1], sc.t[:, p_ * 8:(p_ + 1) * 8], cst.t[:, C_ONES:C_ONES + 1], p_ == 0, False,
                           [sc.r, cst.r], [PR[bd]])
                    MM(ps[bd][0:8, 0:1], sc.t[0:1, 24:32], one1, False, True, [sc.r, cst.r], [PR[bd]])
                    RECIP(den.t[:, 0:1], ps[bd][0:8, 0:1], [PR[bd]], [den.r])
                    TS("dve", num.t[:], num.t[:], den.t[:, 0:1], None, ALU.mult, None, [num.r, den.r], [num.r])
                    for h in range(8):
                        DMA("sp", oar.t[0:1, h * 128:(h + 1) * 128], num.t[h:h + 1, h * 128:(h + 1) * 128], [num.r], [oar.r])
                    ga = tile(a_, "ga", [1, 1024], F32)
                    ACT(ga.t[:], zs.t[0:1, 3072:4096], AF.Silu, [zs.r], [ga.r])
                    TT("dve", mixr.t[0:1, 0:1024], oar.t[:], ga.t[:], ALU.mult, [oar.r, ga.r], [mixr.r])
                    phase_end()
                with contextlib.ExitStack() as b_:
                    ue = tile(b_, "ue", [16, 1024], F32)
                    pc = tile(b_, "pc", [16, 4], F32)
                    ymr = tile(b_, "ymr", [1, 1024], F32)
                    ymc = tile(b_, "ymc", [128, 8], F32)
                    pwf = tile(b_, "pwf", [128, 4, 2, 256], F32)
                    pscr = tile(b_, "pscr", [1, 1024], F32)
                    gbr = tile(b_, "gbr", [1, 1024], F32)
                    obr = tile(b_, "obr", [1, 1024], F32)
                    DMA("sp", ue.t[0:15, :], st_pool[l], (), [ue.r])
                    DMA("sp", ue.t[15:16, :], zs_d[0:1, 4096:5120], (), [ue.r])
                    DMA("sp", pc.t[:], consts[0:16, C_PCOEF:C_PCOEF + 4], (), [pc.r])
                    DMA("sp", pwf.t[:], pool_w[l].rearrange("g (cc p) d -> p g cc d", p=128), (), [pwf.r])
                    DMA("sp", pscr.t[:], psc_row[l:l + 1, :], (), [pscr.r])
                    bA = nbank()
                    bB = nbank()
                    for g in range(4):
                        bk = bA if g < 2 else bB
                        MM(ps[bk][0:1, (g % 2) * 256:(g % 2 + 1) * 256], pc.t[:, g:g + 1], ue.t[:, g * 256:(g + 1) * 256],
                           True, True, [pc.r, ue.r], [PR[bk]])
                    CP("dve", ymr.t[0:1, 0:512], ps[bA][0:1, 0:512], [PR[bA]], [ymr.r])
                    CP("dve", ymr.t[0:1, 512:1024], ps[bB][0:1, 0:512], [PR[bB]], [ymr.r])
                    row2col(lambda c: ymr.t[0:1, c * 128:(c + 1) * 128], 8, ymc, one1, [ymr.r, cst.r])
                    b4 = nbank()
                    b5 = nbank()
                    for g in range(4):
                        bk = b4 if g < 2 else b5
                        for cc in range(2):
                            MM(ps[bk][0:1, (g % 2) * 256:(g % 2 + 1) * 256], ymc.t[:, g * 2 + cc:g * 2 + cc + 1], pwf.t[:, g, cc, :],
                               cc == 0, cc == 1, [ymc.r, pwf.r], [PR[bk]])
                    CP("dve", obr.t[0:1, 0:512], ps[b4][0:1, 0:512], [PR[b4]], [obr.r])
                    CP("dve", obr.t[0:1, 512:1024], ps[b5][0:1, 0:512], [PR[b5]], [obr.r])
                    ACT(gbr.t[:], zs.t[0:1, 5120:6144], AF.Silu, [zs.r], [gbr.r])
                    TT("dve", obr.t[:], obr.t[:], pscr.t[:], ALU.mult, [obr.r, pscr.r], [obr.r])
                    TT("dve", mixr.t[0:1, 1024:2048], obr.t[:], gbr.t[:], ALU.mult, [obr.r, gbr.r], [mixr.r])
                    phase_end()
                with contextlib.ExitStack() as c_:
                    ce = tile(c_, "ce", [4, 6144], F32)
                    cwr = tile(c_, "cwr", [4, 6144], F32)
                    cr = tile(c_, "cr", [1, 6144], F32)
                    DMA("sp", ce.t[0:3, :], st_conv[l], (), [ce.r])
                    DMA("sp", ce.t[3:4, :], zs_d[0:1, 6144:12288], (), [ce.r])
                    DMA("sp", cwr.t[:], conv_w[l], (), [cwr.r])
                    TT("dve", ce.t[:], ce.t[:], cwr.t[:], ALU.mult, [ce.r, cwr.r], [ce.r])
                    for j in range(12):
                        b = nbank()
                        MM(ps[b][0:1, 0:512], cst.t[0:4, C_ONES:C_ONES + 1], ce.t[:, j * 512:(j + 1) * 512], True, True,
                           [cst.r, ce.r], [PR[b]])
                        ACT(cr.t[0:1, j * 512:(j + 1) * 512], ps[b][0:1, 0:512], AF.Silu, [PR[b]], [cr.r])
                    sqr = tile(c_, "sqr", [1, 4096], F32)
                    nrm = tile(c_, "nrm", [1, 32], F32)
                    TT("dve", sqr.t[:], cr.t[0:1, 0:4096], cr.t[0:1, 0:4096], ALU.mult, [cr.r], [sqr.r])
                    RSUM(nrm.t[:], sqr.t[:].rearrange("p (h d) -> p h d", d=128), [sqr.r], [nrm.r])
                    ACT(nrm.t[:], nrm.t[:], AF.Sqrt, [nrm.r], [nrm.r], bias=EPS)
                    RECIP(nrm.t[:], nrm.t[:], [nrm.r], [nrm.r])
                    TS("dve", nrm.t[0:1, 0:16], nrm.t[0:1, 0:16], SCALE_A, None, ALU.mult, None, [nrm.r], [nrm.r])
                    qkn = tile(c_, "qkn", [1, 4096], F32)
                    TT("dve", qkn.t[:].rearrange("p (h d) -> p h d", d=128), cr.t[0:1, 0:4096].rearrange("p (h d) -> p h d", d=128),
                       bass.AP(nrm.t, 0, [[32, 1], [1, 32], [0, 128]]), ALU.mult, [cr.r, nrm.r], [qkn.r])
                    gr = tile(c_, "gr", [1, 64], F32)
                    DMA("sp", gr.t[0:1, 32:48], a_log[l:l + 1, :], (), [gr.r])
                    DMA("sp", gr.t[0:1, 48:64], dt_bias[l:l + 1, :], (), [gr.r])
                    TT("dve", gr.t[0:1, 0:16], zs.t[0:1, 14336:14352], gr.t[0:1, 48:64], ALU.add, [zs.r, gr.r], [gr.r])
                    ACT(gr.t[0:1, 0:16], gr.t[0:1, 0:16], AF.Exp, [gr.r], [gr.r])
                    ACT(gr.t[0:1, 0:16], gr.t[0:1, 0:16], AF.Ln, [gr.r], [gr.r], bias=1.0)
                    ACT(gr.t[0:1, 32:48], gr.t[0:1, 32:48], AF.Exp, [gr.r], [gr.r])
                    STT("dve", gr.t[0:1, 0:16], gr.t[0:1, 0:16], -1.0, gr.t[0:1, 32:48], ALU.mult, ALU.mult, [gr.r], [gr.r])
                    ACT(gr.t[0:1, 0:16], gr.t[0:1, 0:16], AF.Exp, [gr.r], [gr.r])
                    ACT(gr.t[0:1, 16:32], zs.t[0:1, 14352:14368], AF.Sigmoid, [zs.r], [gr.r])
                    egb = tile(c_, "egb", [128, 16], F32)
                    b = nbank()
                    MM(ps[b][:, 0:16], ones_row, gr.t[0:1, 0:16], True, True, [cst.r, gr.r], [PR[b]])
                    CP("dve", egb.t[:], ps[b][:, 0:16], [PR[b]], [egb.r])
                    qc = tile(c_, "qc", [128, 16], F32)
                    kc = tile(c_, "kc", [128, 16], F32)
                    row2col(lambda c: qkn.t[0:1, c * 128:(c + 1) * 128], 16, qc, one1, [qkn.r, cst.r])
                    row2col(lambda c: qkn.t[0:1, 2048 + c * 128:2048 + (c + 1) * 128], 16, kc, one1, [qkn.r, cst.r])
                    Sd = tile(c_, "Sd", [128, 16, 128], F32)
                    DMA("sp", Sd.t[:], st_delta[l].rearrange("h k v -> k h v"), (), [Sd.r])
                    TT("dve", Sd.t[:], Sd.t[:], bass.AP(egb.t, 0, [[16, 128], [1, 16], [0, 128]]), ALU.mult, [Sd.r, egb.r], [Sd.r])
                    dl = tile(c_, "dl", [1, 2048], F32)
                    for q4 in range(4):
                        b = nbank()
                        for hh in range(4):
                            h = q4 * 4 + hh
                            MM(ps[b][0:1, hh * 128:(hh + 1) * 128], kc.t[:, h:h + 1], Sd.t[:, h, :], True, True, [kc.r, Sd.r], [PR[b]])
                        TT("dve", dl.t[0:1, q4 * 512:(q4 + 1) * 512], cr.t[0:1, 4096 + q4 * 512:4096 + (q4 + 1) * 512],
                           ps[b][0:1, 0:512], ALU.subtract, [cr.r, PR[b]], [dl.r])
                    TT("dve", dl.t[:].rearrange("p (h d) -> p h d", d=128), dl.t[:].rearrange("p (h d) -> p h d", d=128),
                       bass.AP(gr.t, 16, [[64, 1], [1, 16], [0, 128]]), ALU.mult, [dl.r, gr.r], [dl.r])
                    for q4 in range(4):
                        b = nbank()
                        for hh in range(4):
                            h = q4 * 4 + hh
                            MM(ps[b][:, hh * 128:(hh + 1) * 128], qkn.t[0:1, 2048 + h * 128:2048 + (h + 1) * 128],
                               dl.t[0:1, h * 128:(h + 1) * 128], True, True, [qkn.r, dl.r], [PR[b]])
                        TT("dve", Sd.t[:, q4 * 4:(q4 + 1) * 4, :], Sd.t[:, q4 * 4:(q4 + 1) * 4, :],
                           ps[b][:, 0:512].rearrange("p (a b) -> p a b", b=128), ALU.add, [Sd.r, PR[b]], [Sd.r])
                    DMA("sp", delta_s[l].rearrange("h k v -> k h v"), Sd.t[:], [Sd.r], ())
                    orow = tile(c_, "orow", [1, 2048], F32)
                    for q4 in range(4):
                        b = nbank()
                        for hh in range(4):
                            h = q4 * 4 + hh
                            MM(ps[b][0:1, hh * 128:(hh + 1) * 128], qc.t[:, h:h + 1], Sd.t[:, h, :], True, True, [qc.r, Sd.r], [PR[b]])
                        CP("dve", orow.t[0:1, q4 * 512:(q4 + 1) * 512], ps[b][0:1, 0:512], [PR[b]], [orow.r])
                    TT("dve", sqr.t[0:1, 0:2048], orow.t[:], orow.t[:], ALU.mult, [orow.r], [sqr.r])
                    RSUM(nrm.t[0:1, 0:16], sqr.t[0:1, 0:2048].rearrange("p (h d) -> p h d", d=128), [sqr.r], [nrm.r])
                    ACT(nrm.t[0:1, 0:16], nrm.t[0:1, 0:16], AF.Sqrt, [nrm.r], [nrm.r], scale=1.0 / 128, bias=EPS)
                    RECIP(nrm.t[0:1, 0:16], nrm.t[0:1, 0:16], [nrm.r], [nrm.r])
                    TT("dve", orow.t[:].rearrange("p (h d) -> p h d", d=128), orow.t[:].rearrange("p (h d) -> p h d", d=128),
                       bass.AP(nrm.t, 0, [[32, 1], [1, 16], [0, 128]]), ALU.mult, [orow.r, nrm.r], [orow.r])
                    dnr = tile(c_, "dnr", [1, 128], F32)
                    DMA("sp", dnr.t[:], dnw_row[l:l + 1, :], (), [dnr.r])
                    TT("dve", orow.t[:].rearrange("p (h d) -> p h d", d=128), orow.t[:].rearrange("p (h d) -> p h d", d=128),
                       bass.AP(dnr.t, 0, [[128, 1], [0, 16], [1, 128]]), ALU.mult, [orow.r, dnr.r], [orow.r])
                    gcr = tile(c_, "gcr", [1, 2048], F32)
                    ACT(gcr.t[:], zs.t[0:1, 12288:14336], AF.Silu, [zs.r], [gcr.r])
                    TT("dve", mixr.t[0:1, 2048:4096], orow.t[:], gcr.t[:], ALU.mult, [orow.r, gcr.r], [mixr.r])
                    row2col(lambda c: mixr.t[0:1, c * 128:(c + 1) * 128], KC, hs_col, identb.t[0:1, 0:1], [mixr.r, identb.r])
                    DMA("sp", mixs_d, mixr.t[:], [mixr.r], ())
                    phase_end()
                phase_end()

        phase_end()
        for l in range(2):
            src = x_p if l == 0 else yl_d[0]
            xs_src = x_s if l == 0 else ys_d[0]
            norm_phase(src, norm_w[l:l + 1, :], hT_d, None, xs_src, None)
            groups = []
            for gi in range(2):
                groups.append((gi * 512, 512, [sink_fm_store(qaT_d, gi * 4, BF16)]))
            for gi in range(2):
                groups.append((1024 + gi * 512, 512, [sink_fm_store(kaT_d, gi * 4, BF16),
                                                      sink_tm_store(wk_p[l], gi * 512)]))
            for gi in range(2):
                groups.append((2048 + gi * 512, 512, [sink_tm_store(wv_p[l], gi * 512)]))
            for gi in range(2):
                groups.append((3072 + gi * 512, 512, [sink_fm_store(gT_d, gi * 4, BF16, gate=True)]))
            for gi in range(2):
                groups.append((4096 + gi * 512, 512, [sink_tm_store(u_d, gi * 512)]))
            for gi in range(2):
                groups.append((5120 + gi * 512, 512, [sink_fm_store(gT_d, 8 + gi * 4, BF16, gate=True)]))
            for gi in range(12):
                groups.append((6144 + gi * 512, 512, [sink_fm_store(qkvT_d, gi * 4, F32),
                                                      sink_tml_rows(conv_p[l], gi * 512)]))
            for gi in range(4):
                groups.append((12288 + gi * 512, 512, [sink_fm_store(gT_d, 16 + gi * 4, BF16, gate=True)]))
            groups.append((14336, 32, [sink_tm_store(ab_d, 0)]))

            def zs_sink(env, b, sr, col0, nco):
                CP("dve", sr.t[0:1, 0:nco], ps[b][0:1, 0:nco], [PR[b]], [sr.r])
                DMA("sp", zs_d[0:1, col0:col0 + nco], sr.t[0:1, 0:nco], [sr.r], ())
            proj_phase(hT_d, w_in[l], groups, zs_sink)
            DMA("sp", pool_p[l], u_d[2033:2048, :], (), ())
            if stop_after == "proj":
                break
            attn_phase(l)
            pool_phase(l)
            gdn_phase(l)
            sample_phase(l)
            ydst = yl_d[l]
            ysd = ys_d[l]
            og = [(gi * 512, 512, [sink_tm_resid(src, ydst)]) for gi in range(8)]

            def ys_sink(env, b, sr, col0, nco, xs_src=xs_src, ysd=ysd):
                xx = env["xr"][0]
                DMA("sp", xx.t[0:1, 0:nco], xs_src[0:1, col0:col0 + nco], (), [xx.r])
                TT("dve", sr.t[0:1, 0:nco], ps[b][0:1, 0:nco], xx.t[0:1, 0:nco], ALU.add, [PR[b], xx.r], [sr.r])
                DMA("sp", ysd[0:1, col0:col0 + nco], sr.t[0:1, 0:nco], [sr.r], ())
            proj_phase(mixT_d, w_out[l], og, ys_sink)
            if stop_after == "l0":
                break
        if stop_after is None:
            norm_phase(yl_d[1], fnw, None, y_p, ys_d[1], y_s)
        phase_end()
    return nc


_NC_CACHE = {}


def kernel(**inputs):
    f = lambda a: np.ascontiguousarray(np.asarray(a, dtype=np.float32))
    x_prompt = f(inputs["x_prompt"])
    x_sample = f(inputs["x_sample"])
    ck = f(inputs["cache_win_k"]).reshape(2, 8, 2048, 1024)
    cv = f(inputs["cache_win_v"]).reshape(2, 8, 2048, 1024)
    st_pool = f(inputs["state_pool"])
    st_conv = f(inputs["state_conv"])
    st_delta = f(inputs["state_delta"])
    conv_w = f(inputs["conv_w"])
    cw_col = np.ascontiguousarray(conv_w.reshape(2, 4, 48, 128).transpose(0, 3, 2, 1))
    pscale = f(inputs["pool_scale"])
    psc_col = np.ascontiguousarray(pscale.reshape(2, 8, 128).transpose(0, 2, 1))
    dnw = f(inputs["delta_norm_w"])
    shared = {
        "norm_w": f(inputs["norm_w"]), "w_in": f(inputs["w_in"]), "cw_col": cw_col, "conv_w": conv_w,
        "a_log": f(inputs["a_log"]), "dt_bias": f(inputs["dt_bias"]),
        "dnw_col": np.ascontiguousarray(dnw.reshape(2, 128, 1)), "dnw_row": dnw,
        "pool_w": f(inputs["pool_w"]), "psc_col": psc_col, "psc_row": pscale,
        "w_out": f(inputs["w_out"]), "fnw": f(inputs["final_norm_w"]).reshape(1, D),
        "consts": make_consts(),
    }
    stop_after = os.environ.get("K_STOP")
    nc = build_nc(stop_after, False)
    in_maps = []
    ncores = int(os.environ.get("K_CORES", "8"))
    for c in range(ncores):
        m = dict(shared)
        m["x_p"] = x_prompt[c % 4]
        m["x_s"] = x_sample[c]
        m["ck"] = np.ascontiguousarray(ck[:, c])
        m["cv"] = np.ascontiguousarray(cv[:, c])
        m["st_pool"] = np.ascontiguousarray(st_pool[:, c])
        m["st_conv"] = np.ascontiguousarray(st_conv[:, c])
        m["st_delta"] = np.ascontiguousarray(st_delta[:, c])
        in_maps.append(m)
    res = run_bass_kernel_spmd(nc, in_maps, core_ids=list(range(ncores)))
    R = list(res.results)
    while len(R) < 8:
        R.append(R[0])

    def stk_p(name, shape):
        return np.stack([np.asarray(R[b][name], dtype=np.float32) for b in range(4)], axis=1).reshape(shape)

    def stk_s(name, shape):
        return np.stack([np.asarray(R[s][name], dtype=np.float32) for s in range(8)], axis=1).reshape(shape)
    y_prompt = np.stack([np.asarray(R[b]["y_p"], dtype=np.float32) for b in range(4)], axis=0)
    y_sample = np.stack([np.asarray(R[s]["y_s"], dtype=np.float32) for s in range(8)], axis=0).reshape(8, 1, D)
    return (y_prompt, y_sample,
            stk_p("wk_p", (2, 4, 2048, 8, 128)), stk_p("wv_p", (2, 4, 2048, 8, 128)),
            stk_p("pool_p", (2, 4, 15, 1024)), stk_p("conv_p", (2, 4, 3, 6144)),
            stk_p("delta_p", (2, 4, 16, 128, 128)),
            stk_s("wk_s", (2, 8, 2048, 8, 128)), stk_s("wv_s", (2, 8, 2048, 8, 128)),
            stk_s("pool_s", (2, 8, 15, 1024)), stk_s("conv_s", (2, 8, 3, 6144)),
            stk_s("delta_s", (2, 8, 16, 128, 128)))
```

```python
import contextlib
import os
import numpy as np
import concourse.bass as bass
import concourse.mybir as mybir
from concourse.bass_utils import run_bass_kernel_spmd

F32 = mybir.dt.float32
BF16 = mybir.dt.bfloat16
AF = mybir.ActivationFunctionType
ALU = mybir.AluOpType
AX = mybir.AxisListType

T = 2048
D = 4096
KC = 32
DIN = 14368
EPS = 1e-6
NEG = -30000.0
SCALE_A = 128 ** -0.5

C_ID = 0
C_ONES = 128
C_TRI = 256
C_MNI = 384
C_MNS = 512
C_MASK = 640
C_DM = C_MASK + 19 * 128
C_PCOEF = C_DM + 12 * 128
NCONST = C_PCOEF + 4


def make_consts():
    c = np.zeros((128, NCONST), np.float32)
    j = np.arange(128)[:, None]
    i = np.arange(128)[None, :]
    c[:, C_ID:C_ID + 128] = (i == j)
    c[:, C_ONES:C_ONES + 128] = 1.0
    c[:, C_TRI:C_TRI + 128] = (j <= i)
    c[:, C_MNI:C_MNI + 128] = np.where(i >= j, 0.0, NEG)
    c[:, C_MNS:C_MNS + 128] = np.where(i > j, 0.0, NEG)
    for blk in range(19):
        dlt = blk - 3
        dist = dlt * 128 + i - j
        m = ((dist >= 0) & (dist <= 128)).astype(np.float32)
        m += ((dist >= 0) & (dist <= 512) & (dist % 4 == 0))
        m += ((dist >= 0) & (dist <= 2048) & (dist % 16 == 0))
        c[:, C_MASK + blk * 128:C_MASK + (blk + 1) * 128] = m
    for g, w in enumerate((2, 4, 8, 16)):
        s = j
        t = i
        cur = ((s <= t) & (s > t - w)).astype(np.float32) / w - (s == t)
        prev = ((s - 128 <= t) & (s - 128 > t - w)).astype(np.float32) / w
        cnt = np.minimum(w, t + 1).astype(np.float32)
        first = ((s <= t) & (s > t - w)).astype(np.float32) / cnt - (s == t)
        c[:, C_DM + (g * 3 + 0) * 128:C_DM + (g * 3 + 1) * 128] = cur
        c[:, C_DM + (g * 3 + 1) * 128:C_DM + (g * 3 + 2) * 128] = prev
        c[:, C_DM + (g * 3 + 2) * 128:C_DM + (g * 3 + 3) * 128] = first
        for s_ in range(16):
            c[s_, C_PCOEF + g] = (1.0 / w if s_ >= 16 - w else 0.0) - (1.0 if s_ == 15 else 0.0)
    return c


class Res:
    __slots__ = ("w", "r")

    def __init__(self):
        self.w = None
        self.r = []


class Tl:
    def __init__(self, t):
        self.t = t
        self.r = Res()


class Sched:
    COMPUTE = ("pe", "act", "dve", "pool")
    QUEUES = ("sp", "pool")

    def __init__(self, nc, es, n_dma_sems=20, n_spare=24):
        self.nc = nc
        self.names = ("pe", "act", "dve", "pool", "sp")
        self.streams = {k: [] for k in self.names}
        self.count = {k: 0 for k in self.COMPUTE}
        self.epoch = {k: 0 for k in self.COMPUTE}
        self.known = {k: {} for k in self.names}
        self.n_dma_sems = n_dma_sems
        self.dma_rr = {q: 0 for q in self.QUEUES}
        self.dma_cnt = {}
        self.sems = {}
        self.spare = [es.enter_context(nc.semaphore("sx%d" % i)) for i in range(n_spare)]
        self.bar = es.enter_context(nc.semaphore("bar"))
        self.nbar = 0
        for q in self.QUEUES:
            for i in range(n_dma_sems):
                self.sems["d_%s_%d" % (q, i)] = es.enter_context(nc.semaphore("d_%s_%d" % (q, i)))
        for k in self.COMPUTE:
            self.sems[k + "#0"] = es.enter_context(nc.semaphore("c_" + k))

    def ckey(self, eng):
        return "%s#%d" % (eng, self.epoch[eng])

    def _deps(self, eng, reads, writes):
        evs = {}
        pek = self.ckey("pe")

        def add(ev):
            if ev is None:
                return
            k, v = ev
            if eng == "pe" and k == pek:
                return
            if evs.get(k, 0) < v:
                evs[k] = v
        for r in reads:
            add(r.w)
        for w in writes:
            add(w.w)
            for e in w.r:
                add(e)
        out = []
        kn = self.known[eng]
        for k, v in evs.items():
            if kn.get(k, 0) < v:
                kn[k] = v
                out.append((k, v))
        return out

    def _commit(self, ev, reads, writes):
        for w in writes:
            w.w = ev
            w.r = []
        for r in reads:
            if r not in writes:
                r.r.append(ev)
                if len(r.r) > 48:
                    best = {}
                    for k, v in r.r:
                        if best.get(k, 0) < v:
                            best[k] = v
                    r.r = list(best.items())

    def op(self, eng, fn, reads=(), writes=()):
        waits = self._deps(eng, reads, writes)
        self.count[eng] += 1
        k = self.ckey(eng)
        ev = (k, self.count[eng])
        self.streams[eng].append((waits, fn, (k, 1)))
        self._commit(ev, reads, writes)
        return ev

    def dma(self, q, fn, reads=(), writes=()):
        i = self.dma_rr[q]
        self.dma_rr[q] = (i + 1) % self.n_dma_sems
        key = "d_%s_%d" % (q, i)
        n = self.dma_cnt.get(key, 0)
        waits = self._deps(q, reads, writes)
        if n > 0 and self.known[q].get(key, 0) < 16 * n:
            self.known[q][key] = 16 * n
            waits.append((key, 16 * n))
        self.dma_cnt[key] = n + 1
        ev = (key, 16 * (n + 1))
        self.streams[q].append((waits, fn, (key, 16)))
        self._commit(ev, reads, writes)
        return ev

    def barrier(self):
        allv = {}
        for k in self.COMPUTE:
            if self.count[k]:
                allv[self.ckey(k)] = self.count[k]
        for key, n in self.dma_cnt.items():
            allv[key] = 16 * n
        waits = []
        for k, v in allv.items():
            if self.known["sp"].get(k, 0) < v:
                waits.append((k, v))
        self.nbar += 1
        nb = self.nbar
        self.streams["sp"].append((waits, None, ("__bar", 1)))
        for e in self.names:
            if e != "sp":
                self.streams[e].append(([("__bar", nb)], None, None))
            for k, v in allv.items():
                self.known[e][k] = v
        for k in self.COMPUTE:
            if self.count[k] > 12000:
                self.epoch[k] += 1
                self.count[k] = 0
                self.sems[self.ckey(k)] = self.spare.pop()

    def flush(self):
        nc = self.nc
        sems = self.sems
        sems["__bar"] = self.bar
        streams = self.streams
        self.streams = {k: [] for k in self.names}

        def run(engh, name):
            for waits, fn, inc in streams[name]:
                for k, v in waits:
                    engh.wait_ge(sems[k], v)
                if fn is None:
                    if inc is not None:
                        engh.sem_inc(sems[inc[0]], inc[1])
                    continue
                ins = fn(engh)
                ins.then_inc(sems[inc[0]], inc[1])

        with nc.Block() as block:
            @block.tensor
            def _(e):
                run(e, "pe")

            @block.scalar
            def _(e):
                run(e, "act")

            @block.vector
            def _(e):
                run(e, "dve")

            @block.gpsimd
            def _(e):
                run(e, "pool")

            @block.sync
            def _(e):
                run(e, "sp")


def build_nc(stop_after=None, debug=False):
    nc = bass.Bass("TRN2", target_bir_lowering=False)

    def din(name, shape):
        return nc.dram_tensor(name, list(shape), F32, kind="ExternalInput").ap()

    def dout(name, shape):
        return nc.dram_tensor(name, list(shape), F32, kind="ExternalOutput").ap()

    def dscr(name, shape, dt):
        if debug and name in ("mixT_d", "y0_d", "ys0_d", "mixs_d"):
            return nc.dram_tensor(name, list(shape), dt, kind="ExternalOutput").ap()
        return nc.dram_tensor(name, list(shape), dt).ap()

    x_p = din("x_p", [T, D])
    x_s = din("x_s", [1, D])
    ck = din("ck", [2, 2048, 1024])
    cv = din("cv", [2, 2048, 1024])
    st_pool = din("st_pool", [2, 15, 1024])
    st_conv = din("st_conv", [2, 3, 6144])
    st_delta = din("st_delta", [2, 16, 128, 128])
    norm_w = din("norm_w", [2, D])
    w_in = din("w_in", [2, D, DIN])
    cw_col = din("cw_col", [2, 128, 48, 4])
    conv_w = din("conv_w", [2, 4, 6144])
    a_log = din("a_log", [2, 16])
    dt_bias = din("dt_bias", [2, 16])
    dnw_col = din("dnw_col", [2, 128, 1])
    dnw_row = din("dnw_row", [2, 128])
    pool_w = din("pool_w", [2, 4, 256, 256])
    psc_col = din("psc_col", [2, 128, 8])
    psc_row = din("psc_row", [2, 1024])
    w_out = din("w_out", [2, D, D])
    fnw = din("fnw", [1, D])
    consts = din("consts", [128, NCONST])

    y_p = dout("y_p", [T, D])
    y_s = dout("y_s", [1, D])
    wk_p = dout("wk_p", [2, T, 1024])
    wv_p = dout("wv_p", [2, T, 1024])
    pool_p = dout("pool_p", [2, 15, 1024])
    conv_p = dout("conv_p", [2, 3, 6144])
    delta_p = dout("delta_p", [2, 16, 128, 128])
    wk_s = dout("wk_s", [2, 2048, 1024])
    wv_s = dout("wv_s", [2, 2048, 1024])
    pool_s = dout("pool_s", [2, 15, 1024])
    conv_s = dout("conv_s", [2, 3, 6144])
    delta_s = dout("delta_s", [2, 16, 128, 128])

    hT_d = dscr("hT_d", [KC, 128, T], BF16)
    qaT_d = dscr("qaT_d", [8, 128, T], BF16)
    kaT_d = dscr("kaT_d", [8, 128, T], BF16)
    gT_d = dscr("gT_d", [32, 128, T], BF16)
    u_d = dscr("u_d", [T, 1024], F32)
    qkvT_d = dscr("qkvT_d", [48, 128, T], F32)
    ab_d = dscr("ab_d", [T, 32], F32)
    mixT_d = dscr("mixT_d", [KC, 128, T], BF16)
    yl_d = [dscr("y0_d", [T, D], F32), dscr("y1_d", [T, D], F32)]
    zs_d = dscr("zs_d", [1, DIN], F32)
    mixs_d = dscr("mixs_d", [1, D], BF16)
    ys_d = [dscr("ys0_d", [1, D], F32), dscr("ys1_d", [1, D], F32)]

    with contextlib.ExitStack() as es:
        S = Sched(nc, es)
        ps = [es.enter_context(nc.psum_tensor("ps%d" % i, [128, 512], F32)) for i in range(8)]
        PR = [Res() for _ in range(8)]
        psb = [p.bitcast(BF16) for p in ps]
        rr = {"ps": 0}

        def nbank(lo=0, hi=8):
            b = lo + rr["ps"] % (hi - lo)
            rr["ps"] += 1
            return b

        def tile(ctx, name, shape, dt):
            rr["tn"] = rr.get("tn", 0) + 1
            return Tl(ctx.enter_context(nc.sbuf_tensor("%s_%d" % (name, rr["tn"]), list(shape), dt)))

        def MM(out, lhsT, rhs, st, sp, reads, writes):
            S.op("pe", lambda e: e.matmul(out, lhsT=lhsT, rhs=rhs, start=st, stop=sp), reads, writes)

        def TR(out, in_, ident, reads, writes):
            S.op("pe", lambda e: e.transpose(out=out, in_=in_, identity=ident), reads, writes)

        def ACT(out, in_, func, reads, writes, **kw):
            S.op("act", lambda e: e.activation(out=out, in_=in_, func=func, **kw), reads, writes)

        def CP(eng, out, in_, reads, writes):
            if eng == "act":
                S.op("act", lambda e: e.copy(out=out, in_=in_), reads, writes)
            else:
                S.op(eng, lambda e: e.tensor_copy(out=out, in_=in_), reads, writes)

        def TT(eng, out, in0, in1, op, reads, writes):
            S.op(eng, lambda e: e.tensor_tensor(out=out, in0=in0, in1=in1, op=op), reads, writes)

        def TS(eng, out, in0, s1, s2, op0, op1, reads, writes):
            if s2 is None:
                S.op(eng, lambda e: e.tensor_scalar(out=out, in0=in0, scalar1=s1, scalar2=None, op0=op0), reads, writes)
            else:
                S.op(eng, lambda e: e.tensor_scalar(out=out, in0=in0, scalar1=s1, scalar2=s2, op0=op0, op1=op1), reads, writes)

        def STT(eng, out, in0, scalar, in1, op0, op1, reads, writes):
            S.op(eng, lambda e: e.scalar_tensor_tensor(out=out, in0=in0, scalar=scalar, in1=in1, op0=op0, op1=op1), reads, writes)

        def RECIP(out, in_, reads, writes):
            S.op("dve", lambda e: e.reciprocal(out=out, in_=in_), reads, writes)

        def MSET(eng, ap, val, writes):
            S.op(eng, lambda e: e.memset(ap, val), (), writes)

        def RSUM(out, in_, reads, writes):
            S.op("dve", lambda e: e.reduce_sum(out=out, in_=in_, axis=AX.X), reads, writes)

        def DMA(q, out, in_, reads=(), writes=()):
            S.dma(q, lambda e: e.dma_start(out=out, in_=in_), reads, writes)

        def phase_end():
            S.barrier()
            S.flush()

        cst = tile(es, "cst", [128, 640], F32)
        identb = tile(es, "identb", [128, 128], BF16)
        onesb = tile(es, "onesb", [128, 128], BF16)
        hs_col = tile(es, "hs_col", [128, KC], BF16)
        DMA("sp", cst.t[:], consts[:, 0:640], (), [cst.r])
        CP("dve", identb.t[:], cst.t[:, C_ID:C_ID + 128], [cst.r], [identb.r])
        CP("dve", onesb.t[:], cst.t[:, C_ONES:C_ONES + 128], [cst.r], [onesb.r])
        identf = cst.t[:, C_ID:C_ID + 128]
        onesf = cst.t[:, C_ONES:C_ONES + 128]
        trif = cst.t[:, C_TRI:C_TRI + 128]
        mnI = cst.t[:, C_MNI:C_MNI + 128]
        mnS = cst.t[:, C_MNS:C_MNS + 128]

        def row2col(row_ap_fn, n, out_tile, one_ap, reads):
            b = nbank()
            for c in range(n):
                MM(ps[b][:, c:c + 1], row_ap_fn(c), one_ap, True, True, reads, [PR[b]])
            CP("dve", out_tile.t[:, 0:n], ps[b][:, 0:n], [PR[b]], [out_tile.r])

        def norm_phase(src, gain_row, hT_out, final_dst, xs_src, xs_dst):
            with contextlib.ExitStack() as ph:
                xt = [tile(ph, "xt%d" % i, [128, D], F32) for i in range(2)]
                gainb = tile(ph, "gainb", [128, D], F32)
                junk = tile(ph, "junk", [128, D], BF16)
                ss = tile(ph, "ss", [128, 40], F32)
                ssr = [Res() for _ in range(17)]
                DMA("sp", gainb.t[:], gain_row.partition_broadcast(128), (), [gainb.r])
                MSET("dve", ss.t[:], 0.0, ssr)
                if final_dst is None:
                    hb = [tile(ph, "hb%d" % i, [128, D], BF16) for i in range(2)]
                    hTt = [tile(ph, "hTt%d" % i, [128, KC, 512], BF16) for i in range(2)]
                else:
                    yo = [tile(ph, "yo%d" % i, [128, D], F32) for i in range(2)]
                for tt in range(17):
                    np_ = 128 if tt < 16 else 1
                    x = xt[tt % 2]
                    if tt < 16:
                        DMA("sp", x.t[:], src[tt * 128:(tt + 1) * 128, :], (), [x.r])
                    else:
                        DMA("sp", x.t[0:1, :], xs_src, (), [x.r])
                    ACT(junk.t[0:np_, :], x.t[0:np_, :], AF.Square, [x.r], [junk.r, ssr[tt]],
                        accum_out=ss.t[0:np_, tt:tt + 1])
                    ACT(ss.t[0:np_, 20 + tt:21 + tt], ss.t[0:np_, tt:tt + 1], AF.Sqrt, [ssr[tt]], [ssr[tt]],
                        scale=1.0 / D, bias=EPS)
                    RECIP(ss.t[0:np_, 20 + tt:21 + tt], ss.t[0:np_, 20 + tt:21 + tt], [ssr[tt]], [ssr[tt]])
                    rstd = ss.t[0:np_, 20 + tt:21 + tt]
                    if final_dst is None:
                        h = hb[tt % 2]
                        STT("dve", h.t[0:np_, :], x.t[0:np_, :], rstd, gainb.t[0:np_, :], ALU.mult, ALU.mult,
                            [x.r, ssr[tt], gainb.r], [h.r])
                        if tt == 16:
                            row2col(lambda c: h.t[0:1, c * 128:(c + 1) * 128], KC, hs_col, identb.t[0:1, 0:1],
                                    [h.r, identb.r])
                            continue
                        ht = hTt[(tt // 4) % 2]
                        for c4 in range(4):
                            b = nbank()
                            for j in range(8):
                                c = c4 * 8 + j
                                TR(psb[b][:, j * 128:(j + 1) * 128], h.t[:, c * 128:(c + 1) * 128], identb.t[:],
                                   [h.r, identb.r], [PR[b]])
                            CP("act" if c4 % 2 == 0 else "dve",
                               ht.t[:, c4 * 8:(c4 + 1) * 8, (tt % 4) * 128:(tt % 4 + 1) * 128],
                               psb[b][:, :].rearrange("p (j t) -> p j t", t=128), [PR[b]], [ht.r])
                        if tt % 4 == 3:
                            tb = tt // 4
                            DMA("pool", hT_out.rearrange("c p t -> p c t")[:, :, tb * 512:(tb + 1) * 512], ht.t[:],
                                [ht.r], ())
                    else:
                        y = yo[tt % 2]
                        STT("dve", y.t[0:np_, :], x.t[0:np_, :], rstd, gainb.t[0:np_, :], ALU.mult, ALU.mult,
                            [x.r, ssr[tt], gainb.r], [y.r])
                        if tt < 16:
                            DMA("pool", final_dst[tt * 128:(tt + 1) * 128, :], y.t[:], [y.r], ())
                        else:
                            DMA("sp", xs_dst, y.t[0:1, :], [y.r], ())
                phase_end()

        def proj_phase(lhs_d, W, groups, sample_dst, resid=None):
            with contextlib.ExitStack() as ph:
                A = tile(ph, "A", [128, KC, 1024], BF16)
                Ar = [Res() for _ in range(4)]
                Wb = [tile(ph, "Wb%d" % i, [128, KC, 512], BF16) for i in range(2)]
                Wr = [[Res() for _ in range(4)] for _ in range(2)]
                stF = [tile(ph, "stF%d" % i, [128, 512], F32) for i in range(4)]
                stB = [tile(ph, "stB%d" % i, [128, 512], BF16) for i in range(4)]
                xr = [tile(ph, "xr%d" % i, [128, 512], F32) for i in range(3)]
                srow = [tile(ph, "srow%d" % i, [1, 512], F32) for i in range(2)]
                cnt = {"f": 0, "b": 0, "x": 0, "s": 0, "w": 0}
                lhs_v = lhs_d.rearrange("c p t -> p c t")
                Wv = W.rearrange("(c p) n -> p c n", p=128)

                def nstF():
                    cnt["f"] += 1
                    return stF[cnt["f"] % 4]

                def nstB():
                    cnt["b"] += 1
                    return stB[cnt["b"] % 4]

                env = dict(nstF=nstF, nstB=nstB, xr=xr, cnt=cnt)
                for half in range(2):
                    for q in range(4):
                        DMA("sp", A.t[:, q * 8:(q + 1) * 8, :],
                            lhs_v[:, q * 8:(q + 1) * 8, half * 1024:(half + 1) * 1024], (), [Ar[q]])
                    for (col0, nco, sinks) in groups:
                        buf = cnt["w"] % 2
                        cnt["w"] += 1
                        w = Wb[buf]
                        for q in range(4):
                            DMA("pool", w.t[:, q * 8:(q + 1) * 8, 0:nco], Wv[:, q * 8:(q + 1) * 8, col0:col0 + nco],
                                (), [Wr[buf][q]])
                        for kind, fn in sinks:
                            if kind == "fm":
                                for fc in range(nco // 128):
                                    for tb in range(2):
                                        b = nbank(0, 6)
                                        for c in range(KC):
                                            MM(ps[b][:, 0:512], w.t[:, c, fc * 128:(fc + 1) * 128],
                                               A.t[:, c, tb * 512:(tb + 1) * 512], c == 0, c == KC - 1,
                                               [Wr[buf][c // 8], Ar[c // 8]], [PR[b]])
                                        fn(env, b, fc, half * 1024 + tb * 512)
                            else:
                                for tt in range(8):
                                    if kind == "tml" and not (half == 1 and tt == 7):
                                        continue
                                    ttg = half * 8 + tt
                                    pre = fn(env, None, ttg, col0, nco) if kind == "tmr" else None
                                    b = nbank(0, 6)
                                    for c in range(KC):
                                        MM(ps[b][:, 0:nco], A.t[:, c, tt * 128:(tt + 1) * 128], w.t[:, c, 0:nco],
                                           c == 0, c == KC - 1, [Wr[buf][c // 8], Ar[c // 8]], [PR[b]])
                                    if kind == "tmr":
                                        fn(env, b, ttg, col0, nco, pre)
                                    else:
                                        fn(env, b, ttg, col0, nco)
                        if half == 0 and sample_dst is not None:
                            b = 6 + cnt["s"] % 2
                            sr = srow[cnt["s"] % 2]
                            cnt["s"] += 1
                            for c in range(KC):
                                MM(ps[b][0:1, 0:nco], hs_col.t[:, c:c + 1], w.t[:, c, 0:nco], c == 0, c == KC - 1,
                                   [Wr[buf][c // 8], hs_col.r], [PR[b]])
                            sample_dst(env, b, sr, col0, nco)
                phase_end()

        def sink_fm_store(dest_d, chunk0, dt, gate=False):
            def fn(env, b, fc, tok0):
                st = env["nstB"]() if dt == BF16 else env["nstF"]()
                if gate:
                    ACT(st.t[:], ps[b][:, 0:512], AF.Silu, [PR[b]], [st.r])
                else:
                    CP("dve", st.t[:], ps[b][:, 0:512], [PR[b]], [st.r])
                DMA("sp", dest_d[chunk0 + fc][:, tok0:tok0 + 512], st.t[:], [st.r], ())
            return ("fm", fn)

        def sink_tm_store(dest2d, dcol0, eng="act"):
            def fn(env, b, ttg, col0, nco):
                st = env["nstF"]()
                CP(eng, st.t[:, 0:nco], ps[b][:, 0:nco], [PR[b]], [st.r])
                DMA("sp", dest2d[ttg * 128:(ttg + 1) * 128, dcol0:dcol0 + nco], st.t[:, 0:nco], [st.r], ())
            return ("tm", fn)

        def sink_tml_rows(dest_rows, dcol0):
            def fn(env, b, ttg, col0, nco):
                st = env["nstF"]()
                CP("act", st.t[:, 0:nco], ps[b][:, 0:nco], [PR[b]], [st.r])
                DMA("sp", dest_rows[:, dcol0:dcol0 + nco], st.t[125:128, 0:nco], [st.r], ())
            return ("tml", fn)

        def sink_tm_resid(x_src, y_dst):
            def fn(env, b, ttg, col0, nco, pre=None):
                if b is None:
                    env["cnt"]["x"] += 1
                    xx = env["xr"][env["cnt"]["x"] % 3]
                    DMA("sp", xx.t[:, 0:nco], x_src[ttg * 128:(ttg + 1) * 128, col0:col0 + nco], (), [xx.r])
                    return xx
                st = env["nstF"]()
                TT("dve", st.t[:, 0:nco], ps[b][:, 0:nco], pre.t[:, 0:nco], ALU.add, [PR[b], pre.r], [st.r])
                DMA("sp", y_dst[ttg * 128:(ttg + 1) * 128, col0:col0 + nco], st.t[:, 0:nco], [st.r], ())
            return ("tmr", fn)

        def attn_phase(l):
            with contextlib.ExitStack() as ph:
                maskb = tile(ph, "maskb", [128, 19 * 128], BF16)
                DMA("pool", maskb.t[:], consts[:, C_MASK:C_MASK + 19 * 128], (), [maskb.r])
                qT = [tile(ph, "qT%d" % i, [128, T], BF16) for i in range(2)]
                kT = [tile(ph, "kT%d" % i, [128, T], BF16) for i in range(2)]
                V = [tile(ph, "V%d" % i, [128, 16, 128], BF16) for i in range(2)]
                gT = [tile(ph, "gT%d" % i, [128, T], BF16) for i in range(2)]
                E = [tile(ph, "E%d" % i, [128, 512], BF16) for i in range(3)]
                P = [tile(ph, "P%d" % i, [128, 512], BF16) for i in range(3)]
                osb = [tile(ph, "osb%d" % i, [128, 512], F32) for i in range(2)]
                rd = [tile(ph, "rd%d" % i, [128, 512], F32) for i in range(2)]
                mx = [tile(ph, "mx%d" % i, [128, 512], BF16) for i in range(2)]
                wvv = wv_p[l].rearrange("(tt p) c -> p tt c", p=128)
                it = 0
                def load_head(hh_):
                    bb_ = hh_ % 2
                    DMA("sp", qT[bb_].t[:], qaT_d[hh_], (), [qT[bb_].r])
                    DMA("sp", kT[bb_].t[:], kaT_d[hh_], (), [kT[bb_].r])
                    DMA("sp", gT[bb_].t[:], gT_d[hh_], (), [gT[bb_].r])
                    DMA("pool", V[bb_].t[:], wvv[:, :, hh_ * 128:(hh_ + 1) * 128], (), [V[bb_].r])
                load_head(0)
                for h in range(8):
                    bf = h % 2
                    if h + 1 < 8:
                        load_head(h + 1)
                    its = [(qg, kt) for qg in range(4) for kt in range(4 * qg + 4)]

                    def emit_st(n):
                        qg_, kt_ = its[n]
                        sbk_ = (it + n) % 4
                        MM(ps[sbk_][:, 0:512], kT[bf].t[:, kt_ * 128:(kt_ + 1) * 128],
                           qT[bf].t[:, qg_ * 512:(qg_ + 1) * 512], True, True, [kT[bf].r, qT[bf].r], [PR[sbk_]])
                    emit_st(0)
                    emit_st(1)
                    for n, (qg, kt) in enumerate(its):
                        ob = 4 + qg % 2
                        db = 6 + qg % 2
                        nk = 4 * qg + 4
                        if True:
                            sbk = (it + n) % 4
                            e = E[(it + n) % 3]
                            p = P[(it + n) % 3]
                            if n + 2 < len(its):
                                emit_st(n + 2)
                            ACT(e.t[:], ps[sbk][:, 0:512], AF.Exp, [PR[sbk]], [e.r], scale=SCALE_A)
                            d0 = 4 * qg - kt + 3
                            TT("dve", p.t[:], e.t[:], maskb.t[:, d0 * 128:(d0 + 4) * 128], ALU.mult,
                               [e.r, maskb.r], [p.r])
                            MM(ps[ob][:, 0:512], V[bf].t[:, kt, :], p.t[:], kt == 0, kt == nk - 1,
                               [V[bf].r, p.r], [PR[ob]])
                            MM(ps[db][:, 0:512], onesb.t[:], p.t[:], kt == 0, kt == nk - 1,
                               [onesb.r, p.r], [PR[db]])
                        if kt != nk - 1:
                            continue
                        r = rd[qg % 2]
                        o = osb[qg % 2]
                        m = mx[qg % 2]
                        RECIP(r.t[:], ps[db][:, 0:512], [PR[db]], [r.r])
                        TT("dve", o.t[:], ps[ob][:, 0:512], r.t[:], ALU.mult, [PR[ob], r.r], [o.r])
                        TT("pool", m.t[:], o.t[:], gT[bf].t[:, qg * 512:(qg + 1) * 512], ALU.mult,
                           [o.r, gT[bf].r], [m.r])
                        DMA("sp", mixT_d[h][:, qg * 512:(qg + 1) * 512], m.t[:], [m.r], ())
                    it += len(its)
                phase_end()

        def pool_phase(l):
            with contextlib.ExitStack() as ph:
                u = tile(ph, "u", [128, 16, 1024], BF16)
                dm = tile(ph, "dm", [128, 12, 128], BF16)
                pw = tile(ph, "pw", [128, 4, 2, 256], BF16)
                psc = tile(ph, "psc", [128, 8], F32)
                gtB = tile(ph, "gtB", [128, 8, T], BF16)
                ym = [[tile(ph, "ym%d_%d" % (cc, k), [128, 512], BF16) for k in range(2)] for cc in range(2)]
                mx = [tile(ph, "pmx%d" % i, [128, 512], BF16) for i in range(3)]
                DMA("pool", u.t[:], u_d.rearrange("(tt p) c -> p tt c", p=128), (), [u.r])
                DMA("pool", dm.t[:], consts[:, C_DM:C_DM + 12 * 128].rearrange("p (a b) -> p a b", b=128), (), [dm.r])
                DMA("pool", pw.t[:], pool_w[l].rearrange("g (cc p) d -> p g cc d", p=128), (), [pw.r])
                DMA("sp", psc.t[:], psc_col[l], (), [psc.r])
                DMA("sp", gtB.t[:], gT_d[8:16].rearrange("c p t -> p c t"), (), [gtB.r])
                it = 0
                for g in range(4):
                    for tb in range(4):
                        for cc in range(2):
                            b = nbank(0, 4)
                            for t4 in range(4):
                                tt = tb * 4 + t4
                                cs = slice(g * 256 + cc * 128, g * 256 + cc * 128 + 128)
                                MM(ps[b][:, t4 * 128:(t4 + 1) * 128], u.t[:, tt, cs],
                                   dm.t[:, g * 3 + (2 if tt == 0 else 0), :], True, tt == 0, [u.r, dm.r], [PR[b]])
                                if tt > 0:
                                    MM(ps[b][:, t4 * 128:(t4 + 1) * 128], u.t[:, tt - 1, cs], dm.t[:, g * 3 + 1, :],
                                       False, True, [u.r, dm.r], [PR[b]])
                            CP("act", ym[cc][it % 2].t[:], ps[b][:, 0:512], [PR[b]], [ym[cc][it % 2].r])
                        for dd in range(2):
                            b2 = 4 + nbank(0, 4)
                            for cc in range(2):
                                MM(ps[b2][:, 0:512], pw.t[:, g, cc, dd * 128:(dd + 1) * 128], ym[cc][it % 2].t[:],
                                   cc == 0, cc == 1, [pw.r, ym[cc][it % 2].r], [PR[b2]])
                            ch = g * 2 + dd
                            m = mx[(it * 2 + dd) % 3]
                            STT("dve", m.t[:], ps[b2][:, 0:512], psc.t[:, ch:ch + 1], gtB.t[:, ch, tb * 512:(tb + 1) * 512],
                                ALU.mult, ALU.mult, [PR[b2], psc.r, gtB.r], [m.r])
                            DMA("sp", mixT_d[8 + ch][:, tb * 512:(tb + 1) * 512], m.t[:], [m.r], ())
                        it += 1
                phase_end()

        def gdn_phase(l):
            with contextlib.ExitStack() as ph:
                cwc = tile(ph, "cwc", [128, 48, 4], F32)
                dnw = tile(ph, "dnw", [128, 1], F32)
                DMA("sp", cwc.t[:], cw_col[l], (), [cwc.r])
                DMA("sp", dnw.t[:], dnw_col[l], (), [dnw.r])
                ab = tile(ph, "ab", [128, 16, 32], F32)
                alb = tile(ph, "alb", [128, 16], F32)
                dtb = tile(ph, "dtb", [128, 16], F32)
                DMA("sp", ab.t[:], ab_d.rearrange("(tt p) c -> p tt c", p=128), (), [ab.r])
                DMA("sp", alb.t[:], a_log[l:l + 1, :].partition_broadcast(128), (), [alb.r])
                DMA("sp", dtb.t[:], dt_bias[l:l + 1, :].partition_broadcast(128), (), [dtb.r])

                def g3(name):
                    return tile(ph, name, [128, 16, 16], F32)
                gg, beta, lnb, gcs, gl, eg, kdw, bw, egl, gbs, tmp = [g3(n) for n in
                    ("gg", "beta", "lnb", "gcs", "gl", "eg", "kdw", "bw", "egl", "gbs", "tmpg")]

                def bc_h(tl):
                    return bass.AP(tl.t, 0, [[16, 128], [0, 16], [1, 16]])
                ACT(alb.t[:], alb.t[:], AF.Exp, [alb.r], [alb.r])
                TT("dve", tmp.t[:], ab.t[:, :, 0:16], bc_h(dtb), ALU.add, [ab.r, dtb.r], [tmp.r])
                ACT(tmp.t[:], tmp.t[:], AF.Exp, [tmp.r], [tmp.r])
                ACT(tmp.t[:], tmp.t[:], AF.Ln, [tmp.r], [tmp.r], bias=1.0)
                STT("dve", gg.t[:], tmp.t[:], -1.0, bc_h(alb), ALU.mult, ALU.mult, [tmp.r, alb.r], [gg.r])
                ACT(beta.t[:], ab.t[:, :, 16:32], AF.Sigmoid, [ab.r], [beta.r])
                ACT(lnb.t[:], beta.t[:], AF.Ln, [beta.r], [lnb.r])
                b = nbank()
                MM(ps[b][:, 0:256], trif, gg.t[:].rearrange("p a b -> p (a b)"), True, True, [cst.r, gg.r], [PR[b]])
                CP("dve", gcs.t[:].rearrange("p a b -> p (a b)"), ps[b][:, 0:256], [PR[b]], [gcs.r])
                b = nbank()
                MM(ps[b][:, 0:256], onesf, gg.t[:].rearrange("p a b -> p (a b)"), True, True, [cst.r, gg.r], [PR[b]])
                CP("dve", gl.t[:].rearrange("p a b -> p (a b)"), ps[b][:, 0:256], [PR[b]], [gl.r])
                ACT(eg.t[:], gcs.t[:], AF.Exp, [gcs.r], [eg.r])
                TT("dve", tmp.t[:], gl.t[:], gcs.t[:], ALU.subtract, [gl.r, gcs.r], [tmp.r])
                ACT(kdw.t[:], tmp.t[:], AF.Exp, [tmp.r], [kdw.r])
                TT("dve", bw.t[:], beta.t[:], eg.t[:], ALU.mult, [beta.r, eg.r], [bw.r])
                ACT(egl.t[:], gl.t[:], AF.Exp, [gl.r], [egl.r])
                TT("dve", gbs.t[:], gcs.t[:], lnb.t[:], ALU.add, [gcs.r, lnb.r], [gbs.r])

                def bc_d(tl, h0, nh, c=None):
                    if c is None:
                        return bass.AP(tl.t, h0, [[256, 128], [16, 16], [0, 128]])
                    return bass.AP(tl.t, c * 16 + h0, [[256, 128], [1, nh], [0, 128]])

                for hg in range(4):
                    h0 = hg * 4
                    with contextlib.ExitStack() as hp:
                        qn = [tile(hp, "qn%d" % i, [128, T], BF16) for i in range(4)]
                        kn = [tile(hp, "kn%d" % i, [128, T], BF16) for i in range(4)]
                        kbg = [tile(hp, "kbg%d" % i, [128, 16, 128], BF16) for i in range(4)]
                        kdec = [tile(hp, "kdec%d" % i, [128, 16, 128], BF16) for i in range(4)]
                        vb = [tile(hp, "vb%d" % i, [128, 16, 128], BF16) for i in range(4)]
                        gC = tile(hp, "gC", [128, 4, T], BF16)
                        DMA("sp", gC.t[:], gT_d[16 + h0:16 + h0 + 4].rearrange("c p t -> p c t"), (), [gC.r])
                        with contextlib.ExitStack() as c1:
                            X = [tile(c1, "X%d" % i, [128, T + 4], BF16) for i in range(3)]
                            Xo = [tile(c1, "Xo%d" % i, [128, T + 4], BF16) for i in range(3)]
                            dg = [tile(c1, "dg%d" % i, [128, 4, 128], BF16) for i in range(3)]
                            yq = tile(c1, "yq", [128, T], F32)
                            sqb = tile(c1, "sqb", [128, T], BF16)
                            vT = tile(c1, "vT", [128, T], BF16)
                            rt = [tile(c1, "rt%d" % i, [128, 512], F32) for i in range(2)]
                            ktm = tile(c1, "ktm", [128, 16, 128], BF16)
                            for xx in X:
                                MSET("dve", xx.t[:, 0:3], 0.0, [xx.r])
                            for xx in Xo:
                                MSET("dve", xx.t[:, 0:4], 0.0, [xx.r])
                            idf4c = bass.AP(cst.t, C_ID, [[640, 128], [0, 4], [1, 128]])
                            k = 0
                            for hh in range(4):
                                h = h0 + hh
                                for fam in range(3):
                                    ch = fam * 16 + h
                                    xx = X[k % 3]
                                    xo = Xo[k % 3]
                                    d_ = dg[k % 3]
                                    k += 1
                                    DMA("pool", xx.t[:, 3:T + 3], qkvT_d[ch], (), [xx.r])
                                    DMA("pool", xo.t[:, 4:T + 4], qkvT_d[ch], (), [xo.r])
                                    TT("dve", d_.t[:], idf4c, bass.AP(cwc.t, ch * 4, [[192, 128], [1, 4], [0, 128]]), ALU.mult,
                                       [cst.r, cwc.r], [d_.r])
                                    for tb in range(4):
                                        b = nbank()
                                        for i in range(4):
                                            if i % 2 == 0:
                                                rhs_ = xx.t[:, tb * 512 + i:tb * 512 + i + 512]
                                            else:
                                                rhs_ = xo.t[:, tb * 512 + i + 1:tb * 512 + i + 1 + 512]
                                            MM(ps[b][:, 0:512], d_.t[:, i, :], rhs_,
                                               i == 0, i == 3, [d_.r, xx.r, xo.r], [PR[b]])
                                        if fam == 2:
                                            ACT(vT.t[:, tb * 512:(tb + 1) * 512], ps[b][:, 0:512], AF.Silu, [PR[b]], [vT.r])
                                        else:
                                            ACT(yq.t[:, tb * 512:(tb + 1) * 512], ps[b][:, 0:512], AF.Silu, [PR[b]], [yq.r])
                                    if fam == 2:
                                        src_t, dsts = vT, [(vb[hh], beta)]
                                    else:
                                        ACT(sqb.t[:], yq.t[:], AF.Square, [yq.r], [sqb.r])
                                        dst = qn[hh] if fam == 0 else kn[hh]
                                        for tb in range(4):
                                            b = nbank()
                                            r_ = rt[tb % 2]
                                            MM(ps[b][:, 0:512], onesb.t[:], sqb.t[:, tb * 512:(tb + 1) * 512], True, True,
                                               [onesb.r, sqb.r], [PR[b]])
                                            ACT(r_.t[:], ps[b][:, 0:512], AF.Ln, [PR[b]], [r_.r], bias=EPS)
                                            ACT(r_.t[:], r_.t[:], AF.Exp, [r_.r], [r_.r], scale=-0.5)
                                            STT("dve", dst.t[:, tb * 512:(tb + 1) * 512], yq.t[:, tb * 512:(tb + 1) * 512],
                                                (SCALE_A if fam == 0 else 1.0), r_.t[:], ALU.mult, ALU.mult,
                                                [yq.r, r_.r], [dst.r])
                                        if fam == 0:
                                            continue
                                        src_t, dsts = kn[hh], [(kbg[hh], bw), (kdec[hh], kdw)]
                                    for half in range(2):
                                        b = nbank()
                                        for j in range(8):
                                            tt = half * 8 + j
                                            TR(psb[b][:, j * 128:(j + 1) * 128], src_t.t[:, tt * 128:(tt + 1) * 128],
                                               identb.t[:], [src_t.r, identb.r], [PR[b]])
                                        CP("act", ktm.t[:, half * 8:(half + 1) * 8, :],
                                           psb[b][:, :].rearrange("p (j t) -> p j t", t=128), [PR[b]], [ktm.r])
                                    for dtile, sc in dsts:
                                        TT("dve", dtile.t[:], ktm.t[:], bc_d(sc, h, 1), ALU.mult, [ktm.r, sc.r], [dtile.r])
                            phase_end()
                        with contextlib.ExitStack() as sc_:
                            def t512(name, dt, n=2):
                                return [tile(sc_, "%s%d" % (name, i), [128, 4, 128], dt) for i in range(n)]
                            Rg = t512("Rg", F32)
                            Rb = t512("Rb", F32)
                            t1 = t512("t1", F32)
                            t2 = t512("t2", F32)
                            E1 = t512("E1", F32)
                            E2 = t512("E2", F32)
                            egr = t512("egr", BF16)
                            qg_ = t512("qg", BF16)
                            qkd = t512("qkd", BF16)
                            NmS = [t512("NmA", BF16, 3), t512("NmB", BF16, 3)]
                            MmS = [t512("MmA", BF16, 3), t512("MmB", BF16, 3)]
                            RmS = [t512("RmA", BF16, 3), t512("RmB", BF16, 3)]
                            usb = t512("usb", F32)
                            wT = t512("wT", BF16)
                            vnew = t512("vnew", BF16)
                            sq_ = t512("sq", BF16)
                            rt_ = t512("rtn", F32)
                            om = t512("om", F32)
                            mxb = t512("mxb", BF16)
                            Sf = tile(sc_, "Sf", [128, 4, 128], F32)
                            Sb = [tile(sc_, "Sb%d" % i, [128, 4, 128], BF16) for i in range(2)]
                            MSET("dve", Sf.t[:], 0.0, [Sf.r])
                            MSET("dve", Sb[0].t[:], 0.0, [Sb[0].r])
                            idf4 = bass.AP(cst.t, C_ID, [[640, 128], [0, 4], [1, 128]])
                            idb4 = bass.AP(identb.t, 0, [[128, 128], [0, 4], [1, 128]])
                            mnI4 = bass.AP(cst.t, C_MNI, [[640, 128], [0, 4], [1, 128]])
                            mnS4 = bass.AP(cst.t, C_MNS, [[640, 128], [0, 4], [1, 128]])

                            def f2(t_):
                                return t_.t[:].rearrange("p a b -> p (a b)")
                            def prep(c):
                                i2 = c % 2
                                cs = slice(c * 128, (c + 1) * 128)
                                Nmm, Mmm, Rmm = NmS[i2], MmS[i2], RmS[i2]
                                TT("pool", Rg[i2].t[:], idf4, bc_d(gcs, h0, 4, c), ALU.mult, [cst.r, gcs.r], [Rg[i2].r])
                                TT("pool", Rb[i2].t[:], idf4, bc_d(gbs, h0, 4, c), ALU.mult, [cst.r, gbs.r], [Rb[i2].r])
                                bg = nbank()
                                MM(ps[bg][:, 0:512], onesf, f2(Rg[i2]), True, True, [cst.r, Rg[i2].r], [PR[bg]])
                                bb = nbank()
                                MM(ps[bb][:, 0:512], onesf, f2(Rb[i2]), True, True, [cst.r, Rb[i2].r], [PR[bb]])
                                yield
                                pg3 = ps[bg][:, 0:512].rearrange("p (a b) -> p a b", b=128)
                                pb3 = ps[bb][:, 0:512].rearrange("p (a b) -> p a b", b=128)
                                TT("dve", t1[i2].t[:], pg3, bc_d(gcs, h0, 4, c), ALU.subtract, [PR[bg], gcs.r], [t1[i2].r])
                                TT("dve", t1[i2].t[:], t1[i2].t[:], mnI4, ALU.add, [t1[i2].r, cst.r], [t1[i2].r])
                                ACT(E1[i2].t[:], t1[i2].t[:], AF.Exp, [t1[i2].r], [E1[i2].r])
                                TT("dve", t2[i2].t[:], pb3, bc_d(gcs, h0, 4, c), ALU.subtract, [PR[bb], gcs.r], [t2[i2].r])
                                TT("dve", t2[i2].t[:], t2[i2].t[:], mnS4, ALU.add, [t2[i2].r, cst.r], [t2[i2].r])
                                ACT(E2[i2].t[:], t2[i2].t[:], AF.Exp, [t2[i2].r], [E2[i2].r])
                                ACT(egr[i2].t[:], pg3, AF.Exp, [PR[bg]], [egr[i2].r])
                                for hh in range(4):
                                    TT("pool", qg_[i2].t[:, hh, :], qn[hh].t[:, cs], egr[i2].t[:, hh, :], ALU.mult,
                                       [qn[hh].r, egr[i2].r], [qg_[i2].r])
                                bG = nbank()
                                bQ = nbank()
                                for hh in range(4):
                                    MM(ps[bG][:, hh * 128:(hh + 1) * 128], kn[hh].t[:, cs], kn[hh].t[:, cs], True, True,
                                       [kn[hh].r], [PR[bG]])
                                    MM(ps[bQ][:, hh * 128:(hh + 1) * 128], kn[hh].t[:, cs], qn[hh].t[:, cs], True, True,
                                       [kn[hh].r, qn[hh].r], [PR[bQ]])
                                yield
                                N = Nmm[0]
                                M = Mmm[0]
                                R = Rmm[0]
                                STT("dve", f2(N), ps[bG][:, 0:512], -1.0, f2(E2[i2]), ALU.mult, ALU.mult,
                                    [PR[bG], E2[i2].r], [N.r])
                                TT("dve", f2(qkd[i2]), ps[bQ][:, 0:512], f2(E1[i2]), ALU.mult, [PR[bQ], E1[i2].r], [qkd[i2].r])
                                bT = nbank()
                                for hh in range(4):
                                    TR(psb[bT][:, hh * 128:(hh + 1) * 128], N.t[:, hh, :], identb.t[:], [N.r, identb.r], [PR[bT]])
                                yield
                                CP("act", f2(M), psb[bT][:, 0:512], [PR[bT]], [M.r])
                                TT("pool", R.t[:], N.t[:], idb4, ALU.add, [N.r, identb.r], [R.r])
                                cur = 0
                                for k in range(1, 7):
                                    nxt = (cur + 1) % 3
                                    N2, M2, R2 = Nmm[nxt], Mmm[nxt], Rmm[nxt]
                                    bM = nbank()
                                    for hh in range(4):
                                        MM(ps[bM][:, hh * 128:(hh + 1) * 128], N.t[:, hh, :], M.t[:, hh, :], True, True,
                                           [N.r, M.r], [PR[bM]])
                                    if k < 6:
                                        bN = nbank()
                                        for hh in range(4):
                                            MM(ps[bN][:, hh * 128:(hh + 1) * 128], M.t[:, hh, :], N.t[:, hh, :], True, True,
                                               [N.r, M.r], [PR[bN]])
                                    yield
                                    CP("act", f2(M2), ps[bM][:, 0:512], [PR[bM]], [M2.r])
                                    if k < 6:
                                        CP("dve", f2(N2), ps[bN][:, 0:512], [PR[bN]], [N2.r])
                                    bR = nbank()
                                    for hh in range(4):
                                        MM(ps[bR][:, hh * 128:(hh + 1) * 128], M2.t[:, hh, :], R.t[:, hh, :], True, True,
                                           [M2.r, R.r], [PR[bR]])
                                    yield
                                    TT("dve", f2(R2), ps[bR][:, 0:512], f2(R), ALU.add, [PR[bR], R.r], [R2.r])
                                    N, M, R = N2, M2, R2
                                    cur = nxt
                                bU = nbank()
                                bW = nbank()
                                for hh in range(4):
                                    MM(ps[bU][:, hh * 128:(hh + 1) * 128], R.t[:, hh, :], vb[hh].t[:, c, :], True, True,
                                       [R.r, vb[hh].r], [PR[bU]])
                                    MM(ps[bW][:, hh * 128:(hh + 1) * 128], kbg[hh].t[:, c, :], R.t[:, hh, :], True, True,
                                       [R.r, kbg[hh].r], [PR[bW]])
                                yield
                                CP("act", f2(usb[i2]), ps[bU][:, 0:512], [PR[bU]], [usb[i2].r])
                                CP("dve", f2(wT[i2]), ps[bW][:, 0:512], [PR[bW]], [wT[i2].r])

                            def scan_step(c):
                                i2 = c % 2
                                cs = slice(c * 128, (c + 1) * 128)
                                Sc = Sb[c % 2]
                                Sn = Sb[(c + 1) % 2]
                                bS = nbank()
                                for hh in range(4):
                                    MM(ps[bS][:, hh * 128:(hh + 1) * 128], wT[i2].t[:, hh, :], Sc.t[:, hh, :], True, True,
                                       [wT[i2].r, Sc.r], [PR[bS]])
                                TT("dve", f2(vnew[i2]), f2(usb[i2]), ps[bS][:, 0:512], ALU.subtract, [usb[i2].r, PR[bS]],
                                   [vnew[i2].r])
                                bO = nbank()
                                bD = nbank()
                                for hh in range(4):
                                    MM(ps[bO][:, hh * 128:(hh + 1) * 128], Sc.t[:, hh, :], qg_[i2].t[:, hh, :], True, False,
                                       [Sc.r, qg_[i2].r], [PR[bO]])
                                    MM(ps[bO][:, hh * 128:(hh + 1) * 128], vnew[i2].t[:, hh, :], qkd[i2].t[:, hh, :], False, True,
                                       [vnew[i2].r, qkd[i2].r], [PR[bO]])
                                    MM(ps[bD][:, hh * 128:(hh + 1) * 128], kdec[hh].t[:, c, :], vnew[i2].t[:, hh, :], True, True,
                                       [kdec[hh].r, vnew[i2].r], [PR[bD]])
                                TT("pool", Sf.t[:], Sf.t[:], bc_d(egl, h0, 4, c), ALU.mult, [Sf.r, egl.r], [Sf.r])
                                TT("dve", f2(Sf), f2(Sf), ps[bD][:, 0:512], ALU.add, [Sf.r, PR[bD]], [Sf.r])
                                CP("act", Sn.t[:], Sf.t[:], [Sf.r], [Sn.r])
                                ACT(f2(sq_[i2]), ps[bO][:, 0:512], AF.Square, [PR[bO]], [sq_[i2].r])
                                bq = nbank()
                                MM(ps[bq][:, 0:512], onesb.t[:], f2(sq_[i2]), True, True, [onesb.r, sq_[i2].r], [PR[bq]])
                                ACT(f2(rt_[i2]), ps[bq][:, 0:512], AF.Sqrt, [PR[bq]], [rt_[i2].r], scale=1.0 / 128, bias=EPS)
                                RECIP(f2(rt_[i2]), f2(rt_[i2]), [rt_[i2].r], [rt_[i2].r])
                                STT("dve", f2(om[i2]), ps[bO][:, 0:512], dnw.t[:, 0:1], f2(rt_[i2]), ALU.mult, ALU.mult,
                                    [PR[bO], dnw.r, rt_[i2].r], [om[i2].r])
                                TT("pool", mxb[i2].t[:], om[i2].t[:], gC.t[:, :, cs], ALU.mult, [om[i2].r, gC.r], [mxb[i2].r])
                                DMA("sp", mixT_d[16 + h0:16 + h0 + 4].rearrange("h p t -> p h t")[:, :, cs], mxb[i2].t[:],
                                    [mxb[i2].r], ())

                            for cp_ in range(8):
                                gens = [prep(2 * cp_), prep(2 * cp_ + 1)]
                                alive = [True, True]
                                while any(alive):
                                    for gi_ in range(2):
                                        if alive[gi_]:
                                            try:
                                                next(gens[gi_])
                                            except StopIteration:
                                                alive[gi_] = False
                                scan_step(2 * cp_)
                                scan_step(2 * cp_ + 1)
                            DMA("sp", delta_p[l, h0:h0 + 4].rearrange("h k v -> k h v"), Sf.t[:], [Sf.r], ())
                            phase_end()
                phase_end()

        def sample_phase(l):
            with contextlib.ExitStack() as ph:
                zs = tile(ph, "zs", [1, DIN], F32)
                DMA("sp", zs.t[:], zs_d, (), [zs.r])
                mixr = tile(ph, "mixr", [1, D], BF16)
                one1 = cst.t[0:1, C_ONES:C_ONES + 1]
                ones_row = cst.t[0:1, C_ONES:C_ONES + 128]
                DMA("sp", wk_s[l, 2047:2048, :], zs.t[0:1, 1024:2048], [zs.r], ())
                DMA("sp", wv_s[l, 2047:2048, :], zs.t[0:1, 2048:3072], [zs.r], ())
                DMA("sp", wk_s[l, 0:2047, :], ck[l, 1:2048, :], (), ())
                DMA("sp", wv_s[l, 0:2047, :], cv[l, 1:2048, :], (), ())
                DMA("sp", pool_s[l, 0:14, :], st_pool[l, 1:15, :], (), ())
                DMA("sp", pool_s[l, 14:15, :], zs.t[0:1, 4096:5120], [zs.r], ())
                DMA("sp", conv_s[l, 0:2, :], st_conv[l, 1:3, :], (), ())
                DMA("sp", conv_s[l, 2:3, :], zs.t[0:1, 6144:12288], [zs.r], ())
                with contextlib.ExitStack() as a_:
                    qb = tile(a_, "qb", [128, 1024], F32)
                    Kp = [tile(a_, "Kp%d" % i, [128, 1024], F32) for i in range(3)]
                    Vp = [tile(a_, "Vp%d" % i, [128, 1024], F32) for i in range(3)]
                    prod = tile(a_, "prod", [128, 1024], F32)
                    sc = tile(a_, "sc", [128, 32], F32)
                    num = tile(a_, "num", [8, 1024], F32)
                    den = tile(a_, "den", [8, 2], F32)
                    oar = tile(a_, "oar", [1, 1024], F32)
                    for hb_ in range(2):
                        b = nbank()
                        MM(ps[b][:, 0:512], ones_row, zs.t[0:1, hb_ * 512:(hb_ + 1) * 512], True, True, [cst.r, zs.r], [PR[b]])
                        CP("dve", qb.t[:, hb_ * 512:(hb_ + 1) * 512], ps[b][:, 0:512], [PR[b]], [qb.r])
                    MSET("dve", sc.t[:], 0.0, [sc.r])
                    for p_, d_ in enumerate((1, 4, 16)):
                        r0 = 2048 - 128 * d_
                        ksrc = ck[l, r0:2048, :].rearrange("(m d) c -> m d c", d=d_)[:, 0, :]
                        vsrc = cv[l, r0:2048, :].rearrange("(m d) c -> m d c", d=d_)[:, 0, :]
                        DMA("sp", Kp[p_].t[:], ksrc, (), [Kp[p_].r])
                        DMA("sp", Vp[p_].t[:], vsrc, (), [Vp[p_].r])
                        TT("dve", prod.t[:], Kp[p_].t[:], qb.t[:], ALU.mult, [Kp[p_].r, qb.r], [prod.r])
                        RSUM(sc.t[:, p_ * 8:(p_ + 1) * 8], prod.t[:].rearrange("p (h d) -> p h d", d=128), [prod.r], [sc.r])
                    TT("dve", prod.t[0:1, :], zs.t[0:1, 1024:2048], zs.t[0:1, 0:1024], ALU.mult, [zs.r], [prod.r])
                    RSUM(sc.t[0:1, 24:32], prod.t[0:1, :].rearrange("p (h d) -> p h d", d=128), [prod.r], [sc.r])
                    ACT(sc.t[:, 0:24], sc.t[:, 0:24], AF.Exp, [sc.r], [sc.r], scale=SCALE_A)
                    ACT(sc.t[0:1, 24:32], sc.t[0:1, 24:32], AF.Exp, [sc.r], [sc.r], scale=SCALE_A)
                    TS("dve", sc.t[0:1, 24:32], sc.t[0:1, 24:32], 3.0, None, ALU.mult, None, [sc.r], [sc.r])
                    b0 = nbank()
                    b1 = nbank()
                    bd = nbank()
                    for hb_, bk in ((0, b0), (1, b1)):
                        for p_ in range(3):
                            MM(ps[bk][0:8, 0:512], sc.t[:, p_ * 8:(p_ + 1) * 8], Vp[p_].t[:, hb_ * 512:(hb_ + 1) * 512],
                               p_ == 0, False, [sc.r, Vp[p_].r], [PR[bk]])
                        MM(ps[bk][0:8, 0:512], sc.t[0:1, 24:32], zs.t[0:1, 2048 + hb_ * 512:2048 + (hb_ + 1) * 512],
                           False, True, [sc.r, zs.r], [PR[bk]])
                        CP("dve", num.t[:, hb_ * 512:(hb_ + 1) * 512], ps[bk][0:8, 0:512], [PR[bk]], [num.r])
                    for p_ in range(3):
                        MM(ps[bd][0:8, 0:1], sc.t[:, p_ * 8:(p_ + 1) * 8], cst.t[:, C_ONES:C_ONES + 1], p_ == 0, False,
                           [sc.r, cst.r], [PR[bd]])
                    MM(ps[bd][0:8, 0:1], sc.t[0:1, 24:32], one1, False, True, [sc.r, cst.r], [PR[bd]])
                    RECIP(den.t[:, 0:1], ps[bd][0:8, 0:1], [PR[bd]], [den.r])
                    TS("dve", num.t[:], num.t[:], den.t[:, 0:1], None, ALU.mult, None, [num.r, den.r], [num.r])
                    for h in range(8):
                        DMA("sp", oar.t[0:1, h * 128:(h + 1) * 128], num.t[h:h + 1, h * 128:(h + 1) * 128], [num.r], [oar.r])
                    ga = tile(a_, "ga", [1, 1024], F32)
                    ACT(ga.t[:], zs.t[0:1, 3072:4096], AF.Silu, [zs.r], [ga.r])
                    TT("dve", mixr.t[0:1, 0:1024], oar.t[:], ga.t[:], ALU.mult, [oar.r, ga.r], [mixr.r])
                    phase_end()
                with contextlib.ExitStack() as b_:
                    ue = tile(b_, "ue", [16, 1024], F32)
                    pc = tile(b_, "pc", [16, 4], F32)
                    ymr = tile(b_, "ymr", [1, 1024], F32)
                    ymc = tile(b_, "ymc", [128, 8], F32)
                    pwf = tile(b_, "pwf", [128, 4, 2, 256], F32)
                    pscr = tile(b_, "pscr", [1, 1024], F32)
                    gbr = tile(b_, "gbr", [1, 1024], F32)
                    obr = tile(b_, "obr", [1, 1024], F32)
                    DMA("sp", ue.t[0:15, :], st_pool[l], (), [ue.r])
                    DMA("sp", ue.t[15:16, :], zs_d[0:1, 4096:5120], (), [ue.r])
                    DMA("sp", pc.t[:], consts[0:16, C_PCOEF:C_PCOEF + 4], (), [pc.r])
                    DMA("sp", pwf.t[:], pool_w[l].rearrange("g (cc p) d -> p g cc d", p=128), (), [pwf.r])
                    DMA("sp", pscr.t[:], psc_row[l:l + 1, :], (), [pscr.r])
                    bA = nbank()
                    bB = nbank()
                    for g in range(4):
                        bk = bA if g < 2 else bB
                        MM(ps[bk][0:1, (g % 2) * 256:(g % 2 + 1) * 256], pc.t[:, g:g + 1], ue.t[:, g * 256:(g + 1) * 256],
                           True, True, [pc.r, ue.r], [PR[bk]])
                    CP("dve", ymr.t[0:1, 0:512], ps[bA][0:1, 0:512], [PR[bA]], [ymr.r])
                    CP("dve", ymr.t[0:1, 512:1024], ps[bB][0:1, 0:512], [PR[bB]], [ymr.r])
                    row2col(lambda c: ymr.t[0:1, c * 128:(c + 1) * 128], 8, ymc, one1, [ymr.r, cst.r])
                    b4 = nbank()
                    b5 = nbank()
                    for g in range(4):
                        bk = b4 if g < 2 else b5
                        for cc in range(2):
                            MM(ps[bk][0:1, (g % 2) * 256:(g % 2 + 1) * 256], ymc.t[:, g * 2 + cc:g * 2 + cc + 1], pwf.t[:, g, cc, :],
                               cc == 0, cc == 1, [ymc.r, pwf.r], [PR[bk]])
                    CP("dve", obr.t[0:1, 0:512], ps[b4][0:1, 0:512], [PR[b4]], [obr.r])
                    CP("dve", obr.t[0:1, 512:1024], ps[b5][0:1, 0:512], [PR[b5]], [obr.r])
                    ACT(gbr.t[:], zs.t[0:1, 5120:6144], AF.Silu, [zs.r], [gbr.r])
                    TT("dve", obr.t[:], obr.t[:], pscr.t[:], ALU.mult, [obr.r, pscr.r], [obr.r])
                    TT("dve", mixr.t[0:1, 1024:2048], obr.t[:], gbr.t[:], ALU.mult, [obr.r, gbr.r], [mixr.r])
                    phase_end()
                with contextlib.ExitStack() as c_:
                    ce = tile(c_, "ce", [4, 6144], F32)
                    cwr = tile(c_, "cwr", [4, 6144], F32)
                    cr = tile(c_, "cr", [1, 6144], F32)
                    DMA("sp", ce.t[0:3, :], st_conv[l], (), [ce.r])
                    DMA("sp", ce.t[3:4, :], zs_d[0:1, 6144:12288], (), [ce.r])
                    DMA("sp", cwr.t[:], conv_w[l], (), [cwr.r])
                    TT("dve", ce.t[:], ce.t[:], cwr.t[:], ALU.mult, [ce.r, cwr.r], [ce.r])
                    for j in range(12):
                        b = nbank()
                        MM(ps[b][0:1, 0:512], cst.t[0:4, C_ONES:C_ONES + 1], ce.t[:, j * 512:(j + 1) * 512], True, True,
                           [cst.r, ce.r], [PR[b]])
                        ACT(cr.t[0:1, j * 512:(j + 1) * 512], ps[b][0:1, 0:512], AF.Silu, [PR[b]], [cr.r])
                    sqr = tile(c_, "sqr", [1, 4096], F32)
                    nrm = tile(c_, "nrm", [1, 32], F32)
                    TT("dve", sqr.t[:], cr.t[0:1, 0:4096], cr.t[0:1, 0:4096], ALU.mult, [cr.r], [sqr.r])
                    RSUM(nrm.t[:], sqr.t[:].rearrange("p (h d) -> p h d", d=128), [sqr.r], [nrm.r])
                    ACT(nrm.t[:], nrm.t[:], AF.Sqrt, [nrm.r], [nrm.r], bias=EPS)
                    RECIP(nrm.t[:], nrm.t[:], [nrm.r], [nrm.r])
                    TS("dve", nrm.t[0:1, 0:16], nrm.t[0:1, 0:16], SCALE_A, None, ALU.mult, None, [nrm.r], [nrm.r])
                    qkn = tile(c_, "qkn", [1, 4096], F32)
                    TT("dve", qkn.t[:].rearrange("p (h d) -> p h d", d=128), cr.t[0:1, 0:4096].rearrange("p (h d) -> p h d", d=128),
                       bass.AP(nrm.t, 0, [[32, 1], [1, 32], [0, 128]]), ALU.mult, [cr.r, nrm.r], [qkn.r])
                    gr = tile(c_, "gr", [1, 64], F32)
                    DMA("sp", gr.t[0:1, 32:48], a_log[l:l + 1, :], (), [gr.r])
                    DMA("sp", gr.t[0:1, 48:64], dt_bias[l:l + 1, :], (), [gr.r])
                    TT("dve", gr.t[0:1, 0:16], zs.t[0:1, 14336:14352], gr.t[0:1, 48:64], ALU.add, [zs.r, gr.r], [gr.r])
                    ACT(gr.t[0:1, 0:16], gr.t[0:1, 0:16], AF.Exp, [gr.r], [gr.r])
                    ACT(gr.t[0:1, 0:16], gr.t[0:1, 0:16], AF.Ln, [gr.r], [gr.r], bias=1.0)
                    ACT(gr.t[0:1, 32:48], gr.t[0:1, 32:48], AF.Exp, [gr.r], [gr.r])
                    STT("dve", gr.t[0:1, 0:16], gr.t[0:1, 0:16], -1.0, gr.t[0:1, 32:48], ALU.mult, ALU.mult, [gr.r], [gr.r])
                    ACT(gr.t[0:1, 0:16], gr.t[0:1, 0:16], AF.Exp, [gr.r], [gr.r])
                    ACT(gr.t[0:1, 16:32], zs.t[0:1, 14352:14368], AF.Sigmoid, [zs.r], [gr.r])
                    egb = tile(c_, "egb", [128, 16], F32)
                    b = nbank()
                    MM(ps[b][:, 0:16], ones_row, gr.t[0:1, 0:16], True, True, [cst.r, gr.r], [PR[b]])
                    CP("dve", egb.t[:], ps[b][:, 0:16], [PR[b]], [egb.r])
                    qc = tile(c_, "qc", [128, 16], F32)
                    kc = tile(c_, "kc", [128, 16], F32)
                    row2col(lambda c: qkn.t[0:1, c * 128:(c + 1) * 128], 16, qc, one1, [qkn.r, cst.r])
                    row2col(lambda c: qkn.t[0:1, 2048 + c * 128:2048 + (c + 1) * 128], 16, kc, one1, [qkn.r, cst.r])
                    Sd = tile(c_, "Sd", [128, 16, 128], F32)
                    DMA("sp", Sd.t[:], st_delta[l].rearrange("h k v -> k h v"), (), [Sd.r])
                    TT("dve", Sd.t[:], Sd.t[:], bass.AP(egb.t, 0, [[16, 128], [1, 16], [0, 128]]), ALU.mult, [Sd.r, egb.r], [Sd.r])
                    dl = tile(c_, "dl", [1, 2048], F32)
                    for q4 in range(4):
                        b = nbank()
                        for hh in range(4):
                            h = q4 * 4 + hh
                            MM(ps[b][0:1, hh * 128:(hh + 1) * 128], kc.t[:, h:h + 1], Sd.t[:, h, :], True, True, [kc.r, Sd.r], [PR[b]])
                        TT("dve", dl.t[0:1, q4 * 512:(q4 + 1) * 512], cr.t[0:1, 4096 + q4 * 512:4096 + (q4 + 1) * 512],
                           ps[b][0:1, 0:512], ALU.subtract, [cr.r, PR[b]], [dl.r])
                    TT("dve", dl.t[:].rearrange("p (h d) -> p h d", d=128), dl.t[:].rearrange("p (h d) -> p h d", d=128),
                       bass.AP(gr.t, 16, [[64, 1], [1, 16], [0, 128]]), ALU.mult, [dl.r, gr.r], [dl.r])
                    for q4 in range(4):
                        b = nbank()
                        for hh in range(4):
                            h = q4 * 4 + hh
                            MM(ps[b][:, hh * 128:(hh + 1) * 128], qkn.t[0:1, 2048 + h * 128:2048 + (h + 1) * 128],
                               dl.t[0:1, h * 128:(h + 1) * 128], True, True, [qkn.r, dl.r], [PR[b]])
                        TT("dve", Sd.t[:, q4 * 4:(q4 + 1) * 4, :], Sd.t[:, q4 * 4:(q4 + 1) * 4, :],
                           ps[b][:, 0:512].rearrange("p (a b) -> p a b", b=128), ALU.add, [Sd.r, PR[b]], [Sd.r])
                    DMA("sp", delta_s[l].rearrange("h k v -> k h v"), Sd.t[:], [Sd.r], ())
                    orow = tile(c_, "orow", [1, 2048], F32)
                    for q4 in range(4):
                        b = nbank()
                        for hh in range(4):
                            h = q4 * 4 + hh
                            MM(ps[b][0:1, hh * 128:(hh + 1) * 128], qc.t[:, h:h + 1], Sd.t[:, h, :], True, True, [qc.r, Sd.r], [PR[b]])
                        CP("dve", orow.t[0:1, q4 * 512:(q4 + 1) * 512], ps[b][0:1, 0:512], [PR[b]], [orow.r])
                    TT("dve", sqr.t[0:1, 0:2048], orow.t[:], orow.t[:], ALU.mult, [orow.r], [sqr.r])
                    RSUM(nrm.t[0:1, 0:16], sqr.t[0:1, 0:2048].rearrange("p (h d) -> p h d", d=128), [sqr.r], [nrm.r])
                    ACT(nrm.t[0:1, 0:16], nrm.t[0:1, 0:16], AF.Sqrt, [nrm.r], [nrm.r], scale=1.0 / 128, bias=EPS)
                    RECIP(nrm.t[0:1, 0:16], nrm.t[0:1, 0:16], [nrm.r], [nrm.r])
                    TT("dve", orow.t[:].rearrange("p (h d) -> p h d", d=128), orow.t[:].rearrange("p (h d) -> p h d", d=128),
                       bass.AP(nrm.t, 0, [[32, 1], [1, 16], [0, 128]]), ALU.mult, [orow.r, nrm.r], [orow.r])
                    dnr = tile(c_, "dnr", [1, 128], F32)
                    DMA("sp", dnr.t[:], dnw_row[l:l + 1, :], (), [dnr.r])
                    TT("dve", orow.t[:].rearrange("p (h d) -> p h d", d=128), orow.t[:].rearrange("p (h d) -> p h d", d=128),
                       bass.AP(dnr.t, 0, [[128, 1], [0, 16], [1, 128]]), ALU.mult, [orow.r, dnr.r], [orow.r])
                    gcr = tile(c_, "gcr", [1, 2048], F32)
                    ACT(gcr.t[:], zs.t[0:1, 12288:14336], AF.Silu, [zs.r], [gcr.r])
                    TT("dve", mixr.t[0:1, 2048:4096], orow.t[:], gcr.t[:], ALU.mult, [orow.r, gcr.r], [mixr.r])
                    row2col(lambda c: mixr.t[0:1, c * 128:(c + 1) * 128], KC, hs_col, identb.t[0:1, 0:1], [mixr.r, identb.r])
                    DMA("sp", mixs_d, mixr.t[:], [mixr.r], ())
                    phase_end()
                phase_end()

        phase_end()
        for l in range(2):
            src = x_p if l == 0 else yl_d[0]
            xs_src = x_s if l == 0 else ys_d[0]
            norm_phase(src, norm_w[l:l + 1, :], hT_d, None, xs_src, None)
            groups = []
            for gi in range(2):
                groups.append((gi * 512, 512, [sink_fm_store(qaT_d, gi * 4, BF16)]))
            for gi in range(2):
                groups.append((1024 + gi * 512, 512, [sink_fm_store(kaT_d, gi * 4, BF16),
                                                      sink_tm_store(wk_p[l], gi * 512)]))
            for gi in range(2):
                groups.append((2048 + gi * 512, 512, [sink_tm_store(wv_p[l], gi * 512)]))
            for gi in range(2):
                groups.append((3072 + gi * 512, 512, [sink_fm_store(gT_d, gi * 4, BF16, gate=True)]))
            for gi in range(2):
                groups.append((4096 + gi * 512, 512, [sink_tm_store(u_d, gi * 512)]))
            for gi in range(2):
                groups.append((5120 + gi * 512, 512, [sink_fm_store(gT_d, 8 + gi * 4, BF16, gate=True)]))
            for gi in range(12):
                groups.append((6144 + gi * 512, 512, [sink_fm_store(qkvT_d, gi * 4, F32),
                                                      sink_tml_rows(conv_p[l], gi * 512)]))
            for gi in range(4):
                groups.append((12288 + gi * 512, 512, [sink_fm_store(gT_d, 16 + gi * 4, BF16, gate=True)]))
            groups.append((14336, 32, [sink_tm_store(ab_d, 0)]))

            def zs_sink(env, b, sr, col0, nco):
                CP("dve", sr.t[0:1, 0:nco], ps[b][0:1, 0:nco], [PR[b]], [sr.r])
                DMA("sp", zs_d[0:1, col0:col0 + nco], sr.t[0:1, 0:nco], [sr.r], ())
            proj_phase(hT_d, w_in[l], groups, zs_sink)
            DMA("sp", pool_p[l], u_d[2033:2048, :], (), ())
            if stop_after == "proj":
                break
            attn_phase(l)
            pool_phase(l)
            gdn_phase(l)
            sample_phase(l)
            ydst = yl_d[l]
            ysd = ys_d[l]
            og = [(gi * 512, 512, [sink_tm_resid(src, ydst)]) for gi in range(8)]

            def ys_sink(env, b, sr, col0, nco, xs_src=xs_src, ysd=ysd):
                xx = env["xr"][0]
                DMA("sp", xx.t[0:1, 0:nco], xs_src[0:1, col0:col0 + nco], (), [xx.r])
                TT("dve", sr.t[0:1, 0:nco], ps[b][0:1, 0:nco], xx.t[0:1, 0:nco], ALU.add, [PR[b], xx.r], [sr.r])
                DMA("sp", ysd[0:1, col0:col0 + nco], sr.t[0:1, 0:nco], [sr.r], ())
            proj_phase(mixT_d, w_out[l], og, ys_sink)
            if stop_after == "l0":
                break
        if stop_after is None:
            norm_phase(yl_d[1], fnw, None, y_p, ys_d[1], y_s)
        phase_end()
    return nc


_NC_CACHE = {}


def kernel(**inputs):
    f = lambda a: np.ascontiguousarray(np.asarray(a, dtype=np.float32))
    x_prompt = f(inputs["x_prompt"])
    x_sample = f(inputs["x_sample"])
    ck = f(inputs["cache_win_k"]).reshape(2, 8, 2048, 1024)
    cv = f(inputs["cache_win_v"]).reshape(2, 8, 2048, 1024)
    st_pool = f(inputs["state_pool"])
    st_conv = f(inputs["state_conv"])
    st_delta = f(inputs["state_delta"])
    conv_w = f(inputs["conv_w"])
    cw_col = np.ascontiguousarray(conv_w.reshape(2, 4, 48, 128).transpose(0, 3, 2, 1))
    pscale = f(inputs["pool_scale"])
    psc_col = np.ascontiguousarray(pscale.reshape(2, 8, 128).transpose(0, 2, 1))
    dnw = f(inputs["delta_norm_w"])
    shared = {
        "norm_w": f(inputs["norm_w"]), "w_in": f(inputs["w_in"]), "cw_col": cw_col, "conv_w": conv_w,
        "a_log": f(inputs["a_log"]), "dt_bias": f(inputs["dt_bias"]),
        "dnw_col": np.ascontiguousarray(dnw.reshape(2, 128, 1)), "dnw_row": dnw,
        "pool_w": f(inputs["pool_w"]), "psc_col": psc_col, "psc_row": pscale,
        "w_out": f(inputs["w_out"]), "fnw": f(inputs["final_norm_w"]).reshape(1, D),
        "consts": make_consts(),
    }
    stop_after = os.environ.get("K_STOP")
    nc = build_nc(stop_after, False)
    in_maps = []
    ncores = int(os.environ.get("K_CORES", "8"))
    for c in range(ncores):
        m = dict(shared)
        m["x_p"] = x_prompt[c % 4]
        m["x_s"] = x_sample[c]
        m["ck"] = np.ascontiguousarray(ck[:, c])
        m["cv"] = np.ascontiguousarray(cv[:, c])
        m["st_pool"] = np.ascontiguousarray(st_pool[:, c])
        m["st_conv"] = np.ascontiguousarray(st_conv[:, c])
        m["st_delta"] = np.ascontiguousarray(st_delta[:, c])
        in_maps.append(m)
    res = run_bass_kernel_spmd(nc, in_maps, core_ids=list(range(ncores)))
    R = list(res.results)
    while len(R) < 8:
        R.append(R[0])

    def stk_p(name, shape):
        return np.stack([np.asarray(R[b][name], dtype=np.float32) for b in range(4)], axis=1).reshape(shape)

    def stk_s(name, shape):
        return np.stack([np.asarray(R[s][name], dtype=np.float32) for s in range(8)], axis=1).reshape(shape)
    y_prompt = np.stack([np.asarray(R[b]["y_p"], dtype=np.float32) for b in range(4)], axis=0)
    y_sample = np.stack([np.asarray(R[s]["y_s"], dtype=np.float32) for s in range(8)], axis=0).reshape(8, 1, D)
    return (y_prompt, y_sample,
            stk_p("wk_p", (2, 4, 2048, 8, 128)), stk_p("wv_p", (2, 4, 2048, 8, 128)),
            stk_p("pool_p", (2, 4, 15, 1024)), stk_p("conv_p", (2, 4, 3, 6144)),
            stk_p("delta_p", (2, 4, 16, 128, 128)),
            stk_s("wk_s", (2, 8, 2048, 8, 128)), stk_s("wv_s", (2, 8, 2048, 8, 128)),
            stk_s("pool_s", (2, 8, 15, 1024)), stk_s("conv_s", (2, 8, 3, 6144)),
            stk_s("delta_s", (2, 8, 16, 128, 128)))
```
